# Optimizing a Trainium2 kernel written in Bass

```python
import jax, jax.numpy as jnp
from jax import lax
import numpy as np

D_MODEL = 1024
BATCH = 8
SEQ = 8192
DEPTH = 1

GRID_W = 64
HEAD_DIM = 64
A_HEADS = D_MODEL // (2 * HEAD_DIM)
A_KV_HEADS = A_HEADS // 4
A_WINDOW = 128
A_BLOCK = 128
B_HEADS = D_MODEL // (2 * HEAD_DIM)
B_KH_MAX = 8
B_KW = 16
D_FF = 4 * D_MODEL
EPS = 1e-6
A_WIDTH = A_HEADS * HEAD_DIM
A_KV_WIDTH = A_KV_HEADS * HEAD_DIM
B_WIDTH = B_HEADS * HEAD_DIM
IN_WIDTH = A_WIDTH + 2 * A_KV_WIDTH + 3 * B_WIDTH + 2 * D_MODEL

kernel_name = "hybrid_window_gqa_natten_gated_encoder"


def rms_norm(x, g):
    xf = x.astype(jnp.float32)
    var = jnp.mean(xf * xf, axis=-1, keepdims=True)
    return (xf * lax.rsqrt(var + EPS)).astype(x.dtype) * g


def alibi_slopes(n):
    return 2.0 ** (-8.0 * jnp.arange(1, n + 1, dtype=jnp.float32) / n)


def windowed_gqa(q, k, v, sink):
    b, s, _, d = q.shape
    nb = s // A_BLOCK
    grp = A_HEADS // A_KV_HEADS
    qb = q.reshape(b, nb, A_BLOCK, A_KV_HEADS, grp, d)
    pad = ((0, 0), (A_BLOCK, A_BLOCK), (0, 0), (0, 0))
    kp = jnp.pad(k, pad)
    vp = jnp.pad(v, pad)
    slopes = alibi_slopes(A_HEADS).reshape(A_KV_HEADS, grp)
    sink_f = sink.astype(jnp.float32).reshape(A_KV_HEADS, grp)
    scale = d ** -0.5

    def block(i):
        qi = qb[:, i]
        ki = lax.dynamic_slice_in_dim(kp, i * A_BLOCK, 3 * A_BLOCK, axis=1)
        vi = lax.dynamic_slice_in_dim(vp, i * A_BLOCK, 3 * A_BLOCK, axis=1)
        q_pos = i * A_BLOCK + jnp.arange(A_BLOCK)
        k_pos = (i - 1) * A_BLOCK + jnp.arange(3 * A_BLOCK)
        dist = jnp.abs(q_pos[:, None] - k_pos[None, :])
        valid = (dist <= A_WINDOW) & (k_pos[None, :] >= 0) & (k_pos[None, :] < s)
        sc = jnp.einsum('bqkgd,bskd->bkgqs', qi, ki).astype(jnp.float32) * scale
        sc = sc - slopes[:, :, None, None] * dist.astype(jnp.float32)
        sc = jnp.where(valid, sc, -jnp.inf)
        sink_col = jnp.broadcast_to(sink_f[None, :, :, None, None], sc.shape[:-1] + (1,))
        p = jax.nn.softmax(jnp.concatenate([sc, sink_col], axis=-1), axis=-1)[..., :-1]
        return jnp.einsum('bkgqs,bskd->bqkgd', p.astype(v.dtype), vi)

    out = lax.map(block, jnp.arange(nb))
    return jnp.moveaxis(out, 0, 1).reshape(b, s, A_WIDTH)


def neighborhood_attn(q, k, v, rpb):
    b, s, h, d = q.shape
    rows = s // GRID_W
    kh = min(B_KH_MAX, rows)
    qg = q.reshape(b, rows, GRID_W, h, d)
    kg = k.reshape(b, rows, GRID_W, h, d)
    vg = v.reshape(b, rows, GRID_W, h, d)
    cols = jnp.arange(GRID_W)
    col_start = jnp.clip(cols - B_KW // 2, 0, GRID_W - B_KW)
    col_idx = col_start[:, None] + jnp.arange(B_KW)[None, :]
    dc = col_idx - cols[:, None] + (B_KW - 1)
    rpb_f = rpb.astype(jnp.float32)
    scale = d ** -0.5

    def row(r):
        r0 = jnp.clip(r - kh // 2, 0, rows - kh)
        kr = lax.dynamic_slice_in_dim(kg, r0, kh, axis=1)
        vr = lax.dynamic_slice_in_dim(vg, r0, kh, axis=1)
        kn = kr[:, :, col_idx]
        vn = vr[:, :, col_idx]
        qr = lax.dynamic_index_in_dim(qg, r, axis=1, keepdims=False)
        sc = jnp.einsum('bchd,brckhd->bhcrk', qr, kn).astype(jnp.float32) * scale
        dr = r0 + jnp.arange(kh) - r + (B_KH_MAX - 1)
        bias = rpb_f[:, dr[:, None, None], dc[None, :, :]]
        sc = sc + jnp.transpose(bias, (0, 2, 1, 3))[None]
        p = jax.nn.softmax(sc.reshape(b, h, GRID_W, kh * B_KW), axis=-1)
        p = p.reshape(b, h, GRID_W, kh, B_KW).astype(v.dtype)
        return jnp.einsum('bhcrk,brckhd->bchd', p, vn)

    out = lax.map(row, jnp.arange(rows))
    return jnp.moveaxis(out, 0, 1).reshape(b, s, B_WIDTH)


def setup_inputs(seed: int = 0) -> dict:
    key = jax.random.key(seed)
    ks = jax.random.split(key, 14)
    f32 = jnp.float32

    def nrm(k, shape, scale):
        return jax.random.normal(k, shape, f32) * scale

    return {
        "x": nrm(ks[0], (BATCH, SEQ, D_MODEL), 1.0),
        "norm_mix": 1.0 + nrm(ks[1], (DEPTH, D_MODEL), 0.02),
        "w_in": nrm(ks[2], (DEPTH, D_MODEL, IN_WIDTH), D_MODEL ** -0.5),
        "b_gate": nrm(ks[3], (DEPTH, 2 * D_MODEL), 0.01),
        "sink": nrm(ks[4], (DEPTH, A_HEADS), 0.5),
        "rpb": nrm(ks[5], (DEPTH, B_HEADS, 2 * B_KH_MAX - 1, 2 * B_KW - 1), 0.1),
        "w_proj_a": nrm(ks[6], (DEPTH, A_WIDTH, D_MODEL), A_WIDTH ** -0.5),
        "w_proj_b": nrm(ks[7], (DEPTH, B_WIDTH, D_MODEL), B_WIDTH ** -0.5),
        "w_out": nrm(ks[8], (DEPTH, D_MODEL, D_MODEL), D_MODEL ** -0.5),
        "norm_mlp": 1.0 + nrm(ks[9], (DEPTH, D_MODEL), 0.02),
        "w_up": nrm(ks[10], (DEPTH, D_MODEL, D_FF), D_MODEL ** -0.5),
        "w_down": nrm(ks[11], (DEPTH, D_FF, D_MODEL), D_FF ** -0.5),
        "norm_final": 1.0 + nrm(ks[12], (D_MODEL,), 0.02),
    }


def reference(x, norm_mix, w_in, b_gate, sink, rpb, w_proj_a, w_proj_b, w_out,
              norm_mlp, w_up, w_down, norm_final):
    b, s, _ = x.shape
    sizes = [A_WIDTH, A_KV_WIDTH, A_KV_WIDTH, B_WIDTH, B_WIDTH, B_WIDTH, D_MODEL, D_MODEL]
    offsets = [int(o) for o in np.cumsum(sizes)[:-1]]
    for l in range(DEPTH):
        h = rms_norm(x, norm_mix[l])
        z = h @ w_in[l]
        q_a, k_a, v_a, q_b, k_b, v_b, g_a, g_b = jnp.split(z, offsets, axis=-1)
        y_a = windowed_gqa(q_a.reshape(b, s, A_HEADS, HEAD_DIM),
                           k_a.reshape(b, s, A_KV_HEADS, HEAD_DIM),
                           v_a.reshape(b, s, A_KV_HEADS, HEAD_DIM), sink[l])
        y_b = neighborhood_attn(q_b.reshape(b, s, B_HEADS, HEAD_DIM),
                                k_b.reshape(b, s, B_HEADS, HEAD_DIM),
                                v_b.reshape(b, s, B_HEADS, HEAD_DIM), rpb[l])
        gates = jax.nn.sigmoid((jnp.concatenate([g_a, g_b], axis=-1) + b_gate[l])
                               .astype(jnp.float32)).astype(x.dtype)
        merged = (gates[..., :D_MODEL] * (y_a @ w_proj_a[l])
                  + gates[..., D_MODEL:] * (y_b @ w_proj_b[l]))
        x = x + merged @ w_out[l]
        h2 = rms_norm(x, norm_mlp[l])
        x = x + jnp.square(jax.nn.relu(h2 @ w_up[l])) @ w_down[l]
    return rms_norm(x, norm_final)
```

```python
import os
import numpy as np
import concourse.bass as bass
import concourse.mybir as mybir
from concourse.bass_utils import run_bass_kernel_spmd

F32 = mybir.dt.float32
BF16 = mybir.dt.bfloat16
AF = mybir.ActivationFunctionType
ALU = mybir.AluOpType

D = 1024
KC = 8
NIN = 4352
DFF = 4096
QA, KA, QB, KB, VA, VB, GA, GB = 0, 512, 640, 1152, 1664, 1792, 2304, 3328
NEG = -30000.0
EPS = 1e-6
T1 = 256
T2 = 512
ENGS = ("pe", "act", "dve", "pool", "sp")


class Prog:
    def __init__(self, same_engine_sync=True):
        self.ops = {e: [] for e in ENGS}
        self.ncomp = {e: 0 for e in ENGS}
        self.chan_cnt = {}
        self.lastw = {}
        self.readers = {}
        self.waited = {e: {} for e in ENGS}
        self.signals = set()
        self.same_engine_sync = same_engine_sync

    def _deps(self, reads, writes):
        deps = {}
        for r in reads:
            w = self.lastw.get(r)
            if w is not None:
                deps[w] = True
        for w_ in writes:
            w = self.lastw.get(w_)
            if w is not None:
                deps.setdefault(w, False)
            for c, i in self.readers.get(w_, {}).items():
                deps.setdefault((c, i), False)
        return deps

    def _waits(self, eng, deps):
        if not isinstance(deps, dict):
            deps = {d_: True for d_ in deps}
        best = {}
        for (c, i), raw in deps.items():
            if c == eng and (eng == "pe" or not self.same_engine_sync or not raw):
                continue
            if best.get(c, 0) < i:
                best[c] = i
        waits = []
        for c, i in best.items():
            if self.waited[eng].get(c, 0) >= i:
                continue
            self.waited[eng][c] = i
            waits.append((c, i))
            if not c.startswith("dma:"):
                self.signals.add((c, i))
        return waits

    def _commit(self, counter, idx, reads, writes):
        for r in reads:
            self.readers.setdefault(r, {})[counter] = idx
        for w_ in writes:
            self.lastw[w_] = (counter, idx)
            self.readers[w_] = {}

    def op(self, eng, fn, reads=(), writes=()):
        waits = self._waits(eng, self._deps(reads, writes))
        self.ncomp[eng] += 1
        idx = self.ncomp[eng]
        self.ops[eng].append(dict(fn=fn, waits=waits, kind="c", idx=idx))
        self._commit(eng, idx, reads, writes)

    def dma(self, queue, chan, fn, reads=(), writes=()):
        waits = self._waits(queue, self._deps(reads, writes))
        c = "dma:" + chan
        self.chan_cnt[c] = self.chan_cnt.get(c, 0) + 1
        idx = self.chan_cnt[c]
        self.ops[queue].append(dict(fn=fn, waits=waits, kind="d", chan=c))
        self._commit(c, idx, reads, writes)

    def wait_all(self, eng, include_dma=True):
        deps = set()
        for e in ENGS:
            if e != eng and self.ncomp[e] > 0:
                deps.add((e, self.ncomp[e]))
        if include_dma:
            for c, n in self.chan_cnt.items():
                deps.add((c, n))
        waits = self._waits(eng, deps)
        self.ops[eng].append(dict(fn=None, waits=waits, kind="w"))

    def barrier(self):
        snap = {e: self.ncomp[e] for e in ENGS}
        chans = dict(self.chan_cnt)
        for eng in ENGS:
            deps = set((e, n) for e, n in snap.items() if e != eng and n > 0)
            deps |= set(chans.items())
            waits = self._waits(eng, deps)
            self.ops[eng].append(dict(fn=None, waits=waits, kind="w"))

    def replay(self, nc):
        chan_names = sorted(self.chan_cnt.keys())
        semval = {}
        for e in ENGS:
            c = 0
            for i in range(1, self.ncomp[e] + 1):
                if (e, i) in self.signals:
                    c += 1
                semval[(e, i)] = c
        import contextlib
        with contextlib.ExitStack() as st:
            sems = {}
            for e in ENGS:
                sems[e] = st.enter_context(nc.semaphore("s_" + e))
            for c in chan_names:
                sems[c] = st.enter_context(nc.semaphore("s_" + c.replace(":", "_")))
            block = st.enter_context(nc.Block())

            def run(engname, engobj):
                for o in self.ops[engname]:
                    for (c, i) in o["waits"]:
                        if c.startswith("dma:"):
                            engobj.wait_ge(sems[c], 16 * i)
                        else:
                            engobj.wait_ge(sems[c], semval[(c, i)])
                    if o["fn"] is None:
                        continue
                    ins = o["fn"](engobj)
                    if o["kind"] == "d":
                        ins.then_inc(sems[o["chan"]], 16)
                    elif (engname, o["idx"]) in self.signals:
                        ins.then_inc(sems[engname], 1)

            @block.tensor
            def _(e):
                run("pe", e)

            @block.scalar
            def _(e):
                run("act", e)

            @block.vector
            def _(e):
                run("dve", e)

            @block.gpsimd
            def _(e):
                run("pool", e)

            @block.sync
            def _(e):
                run("sp", e)


class Arena:
    def __init__(self, handle, nfloats):
        self.h = handle
        self.n = nfloats
        self.off = 0

    def mark(self):
        return self.off

    def reset(self, m):
        self.off = m

    def alloc(self, dtype, shape):
        nel = int(np.prod(shape))
        nbytes = nel * (2 if dtype == BF16 else 4)
        nfl = (nbytes + 3) // 4
        nfl = (nfl + 7) // 8 * 8
        assert self.off + nfl <= self.n, f"SBUF arena overflow: need {self.off + nfl} > {self.n}"
        ap = self.h[:, self.off:self.off + nfl]
        self.off += nfl
        if dtype == BF16:
            ap = ap.bitcast(BF16)
        ap = ap[:, 0:nel]
        if len(shape) == 2:
            ap = ap.rearrange("p (a b) -> p a b", a=shape[0])
        elif len(shape) == 3:
            ap = ap.rearrange("p (a b c) -> p a b c", a=shape[0], b=shape[1])
        elif len(shape) == 4:
            ap = ap.rearrange("p (a b c d) -> p a b c d", a=shape[0], b=shape[1], c=shape[2])
        return ap


class _Stop(Exception):
    pass


def build_program(S, debug=False, stop_after=None):
    assert S % T2 == 0 and S >= 1024
    NBLK = S // 128
    NS1 = S // T1
    NS2 = S // T2
    NB1 = T1 // 128
    NB2 = T2 // 128

    nc = bass.Bass("TRN2", target_bir_lowering=False)

    def din(name, shape):
        return nc.dram_tensor(name, list(shape), F32, kind="ExternalInput").ap()

    x_d = din("x", [S, D])
    win_d = din("w_in", [D, NIN])
    wpa_d = din("w_proj_a", [512, D])
    wpb_d = din("w_proj_b", [512, D])
    wout_d = din("w_out", [D, D])
    wup_d = din("w_up", [D, DFF])
    wdn_d = din("w_down", [DFF, D])
    gmix_d = din("gmix", [128, KC])
    gmlp_d = din("gmlp", [128, KC])
    bgate_d = din("bgate", [128, 16])
    sink_d = din("sink", [8])
    gfin_d = din("gfin", [D])
    ident_d = din("ident", [128, 128])
    biasA_d = din("biasA", [128, 8 * 3 * 128])
    biasBi_d = din("biasBi", [128, 8 * 5 * 128])
    biasBs_d = din("biasBs", [4, 128, 8 * 4 * 128])
    out_d = nc.dram_tensor("out", [S, D], F32, kind="ExternalOutput").ap()
    x1_d = nc.dram_tensor("x1s", [S, D], F32, kind="ExternalOutput" if debug else "Internal").ap()

    if debug:
        ydbg_d = nc.dram_tensor("ydbg", [S, D], F32, kind="ExternalOutput").ap()
        mdbg_d = nc.dram_tensor("mdbg", [D, S], F32, kind="ExternalOutput").ap()
    P = Prog()

    NFL = 212800 // 4
    arena_h = nc.alloc_sbuf_tensor("arena", [128, NFL], F32)
    AR = Arena(arena_h, NFL)
    banks = [nc.alloc_psum_tensor(f"ps{i}", [128, 512], F32) for i in range(8)]

    def bank_f32(b):
        return banks[b][:, :]

    def bank_bf16(b):
        return banks[b][:, :].bitcast(BF16)

    ident = AR.alloc(BF16, [128])
    gmix = AR.alloc(F32, [KC])
    gmlp = AR.alloc(F32, [KC])
    bgate = AR.alloc(F32, [16])
    es = AR.alloc(F32, [8])
    sinkb = AR.alloc(F32, [8])
    ss = AR.alloc(F32, [16])
    rstd = AR.alloc(F32, [16])
    den = AR.alloc(F32, [8, 4])
    base_mark = AR.mark()

    win = AR.alloc(BF16, [KC, NIN])
    wpa = AR.alloc(BF16, [4, D])
    wpb = AR.alloc(BF16, [4, D])
    wout = AR.alloc(BF16, [KC, D])
    hT = [AR.alloc(BF16, [KC, T1]) for _ in range(3)]
    kT = [AR.alloc(BF16, [5, T1]) for _ in range(3)]
    vR = [AR.alloc(BF16, [NB1, 10, 65]) for _ in range(3)]
    qT = AR.alloc(BF16, [8, T1])
    maskA = AR.alloc(BF16, [8, 3, 128])
    maskBi = AR.alloc(BF16, [8, 5, 128])
    maskBs = AR.alloc(BF16, [8, 4, 128])
    NPU, NPT = 4, 6
    pu = [AR.alloc(BF16, [512]) for _ in range(NPU)]
    pt = [AR.alloc(BF16, [4, 128]) for _ in range(NPT)]
    ysb = [AR.alloc(BF16, [D]) for _ in range(2)]
    yT = [AR.alloc(BF16, [8, T1]) for _ in range(2)]
    sg = [AR.alloc(F32, [512]) for _ in range(2)]
    mT = AR.alloc(BF16, [KC, T1])
    NXP = 4
    xp = [AR.alloc(F32, [D]) for _ in range(NXP)]
    hb = [AR.alloc(BF16, [D]) for _ in range(2)]
    zt = AR.alloc(BF16, [512])

    class RR:
        def __init__(self, n):
            self.n, self.i = n, 0

        def next(self):
            v = self.i % self.n
            self.i += 1
            return v

    rr_xp = RR(NXP)
    rr_hb = RR(2)
    rr_ss = RR(16)
    class RRB:
        def __init__(self):
            self.i = 0

        def next(self):
            v = self.i % 8
            self.i += 1
            return v

    rr_bank = RRB()
    rr_pu = RR(NPU)
    rr_pt = RR(NPT)
    rr_den = RR(8)
    rr_alt = RR(2)

    def emit_rstd(si):
        r = rstd[:, si:si + 1]
        P.op("dve", lambda e, o=r, i=ss[:, si:si + 1]: e.tensor_scalar(o, i, 1.0 / D, EPS, ALU.mult, ALU.add),
             [("ss", si)], [("rstd", si)])
        P.op("act", lambda e, o=r: e.sqrt(o, o), [("rstd", si)], [("rstd", si)])
        P.op("dve", lambda e, o=r: e.reciprocal(o, o), [("rstd", si)], [("rstd", si)])

    def dma_load(buf_ap, src_ap, chan, wkeys, rkeys=(), queue="sp"):
        P.dma(queue, chan, lambda e, o=buf_ap, i=src_ap: e.dma_start(out=o, in_=i), reads=rkeys, writes=wkeys)

    def evac_copy(eng, out_ap, in_ap, rkeys, wkeys, mul=None):
        if eng == "act":
            if mul is None:
                P.op("act", lambda e, o=out_ap, i=in_ap: e.copy(o, i), rkeys, wkeys)
            else:
                P.op("act", lambda e, o=out_ap, i=in_ap, m=mul: e.mul(o, i, m), rkeys, wkeys)
        else:
            if mul is None:
                P.op(eng, lambda e, o=out_ap, i=in_ap: e.tensor_copy(o, i), rkeys, wkeys)
            else:
                P.op(eng, lambda e, o=out_ap, i=in_ap, m=mul: e.tensor_scalar(o, i, m, None, ALU.mult), rkeys, wkeys)

    dma_load(gmix, gmix_d, "c_gmix", ["gmix"])
    dma_load(gmlp, gmlp_d, "c_gmlp", ["gmlp"])
    dma_load(bgate, bgate_d, "c_bgate", ["bgate"])
    dma_load(sinkb, sink_d.partition_broadcast(128), "c_sink", ["sinkb"])
    P.dma("pool", "c_ident", lambda e: e.dma_start(out=ident, in_=ident_d), writes=["ident"])
    P.op("act", lambda e: e.activation(es, sinkb, AF.Exp), ["sinkb"], ["es"])
    P.op("pool", lambda e: e.memset(zt, 0.0), [], ["zt"])
    for r in range(3):
        P.op("pool", lambda e, r=r: e.memset(vR[r].rearrange("p a b c -> p (a b c)"), 1.0), [], [("v", r, b) for b in range(NB1)])

    def load_mask(mask_ap, src_ap, ncols, key):
        mflat = mask_ap.rearrange("p a b c -> p (a b c)")
        for c0 in range(0, ncols, 1024):
            xi = rr_xp.next()
            dma_load(xp[xi], src_ap[:, c0:c0 + 1024], f"ld_x{xi}", [("xp", xi)])
            P.op("act", lambda e, o=mflat[:, c0:c0 + 1024], i=xp[xi]: e.activation(o, i, AF.Exp),
                 [("xp", xi)], [key])

    load_mask(maskA, biasA_d, 8 * 3 * 128, "maskA")
    load_mask(maskBi, biasBi_d, 8 * 5 * 128, "maskBi")
    load_mask(maskBs, biasBs_d[0], 8 * 4 * 128, "maskBs")

    win_v = win_d.rearrange("(kc p) n -> p kc n", p=128)
    pieces = [(0, 1024), (1024, 1024), (2048, 1024), (3072, 1024), (4096, 256)]

    def fold_scale(eng, o, i_, g):
        if eng == "act":
            P_fn = lambda e, o=o, i_=i_, g=g: e.mul(o, i_, g)
        else:
            P_fn = lambda e, o=o, i_=i_, g=g: e.tensor_scalar(o, i_, g, None, ALU.mult)
        return P_fn

    for cb in (1, 0, 2, 3, 4):
        c0, w = pieces[cb]
        for kc in range(KC):
            xi = rr_xp.next()
            dma_load(xp[xi][:, 0:w], win_v[:, kc, c0:c0 + w], f"ld_x{xi}", [("xp", xi)])
            eng = "dve" if rr_alt.next() == 0 else "act"
            P.op(eng, fold_scale(eng, win[:, kc, c0:c0 + w], xp[xi][:, 0:w], gmix[:, kc:kc + 1]),
                 [("xp", xi), "gmix"], [("win", cb)])
    for kc in range(4):
        P.dma("pool", "w_a", lambda e, kc=kc: e.dma_start(out=wpa[:, kc, :], in_=wpa_d[kc * 128:(kc + 1) * 128, :]), reads=[("win", 4)], writes=["wpa"])
        P.dma("pool", "w_b", lambda e, kc=kc: e.dma_start(out=wpb[:, kc, :], in_=wpb_d[kc * 128:(kc + 1) * 128, :]), reads=[("win", 4)], writes=["wpb"])
    for kc in range(KC):
        P.dma("pool", "w_o", lambda e, kc=kc: e.dma_start(out=wout[:, kc, :], in_=wout_d[kc * 128:(kc + 1) * 128, :]), reads=[("win", 4)], writes=["wout"])

    def win_keys(c0, c1):
        return [("win", cb) for cb in range(5) if pieces[cb][0] < c1 and pieces[cb][0] + pieces[cb][1] > c0]

    prep_state = {}

    def norm_prep(x_ap, x_key, hbufs, hkey, tag):
        hi = rr_hb.next()
        si = rr_ss.next()
        P.op("act", lambda e, o=hbufs[hi], i=x_ap, a=ss[:, si:si + 1]: e.activation(o, i, AF.Square, accum_out=a),
             [x_key], [(hkey, hi), ("ss", si)])
        emit_rstd(si)
        P.op("act", lambda e, o=hbufs[hi], i=x_ap, m=rstd[:, si:si + 1]: e.mul(o, i, m),
             [x_key, ("rstd", si)], [(hkey, hi)])
        prep_state[tag] = hi

    def norm_trans(hbufs, hkey, tag, dst_hT, col0, hT_key):
        hi = prep_state.pop(tag)
        b = rr_bank.next()
        pst = bank_bf16(b)
        for kc in range(KC):
            P.op("pe", lambda e, o=pst[:, kc * 128:(kc + 1) * 128], i=hbufs[hi][:, kc * 128:(kc + 1) * 128]:
                 e.transpose(o, i, ident), [(hkey, hi), "ident"], [("bank", b)])
        eng = "dve" if rr_alt.next() == 0 else "act"
        evac_copy(eng, dst_hT[:, :, col0:col0 + 128], pst.rearrange("p (a b) -> p a b", a=KC),
                  [("bank", b)], [hT_key])

    def stage_prep1(s):
        for b in range(NB1):
            r0 = s * T1 + b * 128
            xi = rr_xp.next()
            dma_load(xp[xi], x_d[r0:r0 + 128, :], f"ld_x{xi}", [("xp", xi)])
            norm_prep(xp[xi], ("xp", xi), hb, "hb", ("p1", s, b))

    def stage_trans1(s):
        slot = s % 3
        for b in range(NB1):
            norm_trans(hb, "hb", ("p1", s, b), hT[slot], b * 128, ("hT", slot, b))

    def hT_keys(slot):
        return [("hT", slot, b) for b in range(NB1)]

    def stage_kv(s):
        slot = s % 3
        rs = s % 3
        kchunks = [KA // 128] + [KB // 128 + j for j in range(4)]
        for ki, f in enumerate(kchunks):
            b = rr_bank.next()
            ps = bank_f32(b)
            for kc in range(KC):
                P.op("pe", lambda e, o=ps[:, 0:T1], l=win[:, kc, f * 128:(f + 1) * 128], r=hT[slot][:, kc, :], kc=kc:
                     e.matmul(o, l, r, start=(kc == 0), stop=(kc == KC - 1)),
                     win_keys(f * 128, f * 128 + 128) + hT_keys(slot), [("bank", b)])
            eng = "act" if rr_alt.next() == 0 else "dve"
            evac_copy(eng, kT[rs][:, ki, :], ps[:, 0:T1], [("bank", b)], [("kT", rs, ki)], mul=0.125)
        for blk in range(NB1):
            b1 = rr_bank.next()
            b2 = rr_bank.next()
            psb = bank_f32(b1)
            psa = bank_f32(b2)
            for kc in range(KC):
                P.op("pe", lambda e, o=psb, l=hT[slot][:, kc, blk * 128:(blk + 1) * 128], r=win[:, kc, VB:VB + 512], kc=kc:
                     e.matmul(o, l, r, start=(kc == 0), stop=(kc == KC - 1)),
                     win_keys(VB, VB + 512) + hT_keys(slot), [("bank", b1)])
            for kc in range(KC):
                P.op("pe", lambda e, o=psa[:, 0:128], l=hT[slot][:, kc, blk * 128:(blk + 1) * 128], r=win[:, kc, VA:VA + 128], kc=kc:
                     e.matmul(o, l, r, start=(kc == 0), stop=(kc == KC - 1)),
                     win_keys(VA, VA + 128) + hT_keys(slot), [("bank", b2)])
            evac_copy("dve", vR[rs][:, blk, 2:10, 0:64], psb.rearrange("p (h d) -> p h d", h=8),
                      [("bank", b1)], [("v", rs, blk)])
            evac_copy("act", vR[rs][:, blk, 0:2, 0:64], psa[:, 0:128].rearrange("p (h d) -> p h d", h=2),
                      [("bank", b2)], [("v", rs, blk)])

    def stage_q(s):
        slot = s % 3
        qchunks = [QA // 128 + j for j in range(4)] + [QB // 128 + j for j in range(4)]
        for qi, f in enumerate(qchunks):
            b = rr_bank.next()
            ps = bank_f32(b)
            for kc in range(KC):
                P.op("pe", lambda e, o=ps[:, 0:T1], l=win[:, kc, f * 128:(f + 1) * 128], r=hT[slot][:, kc, :], kc=kc:
                     e.matmul(o, l, r, start=(kc == 0), stop=(kc == KC - 1)),
                     win_keys(f * 128, f * 128 + 128) + hT_keys(slot), [("bank", b)])
            eng = "act" if rr_alt.next() == 0 else "dve"
            evac_copy(eng, qT[:, qi, :], ps[:, 0:T1], [("bank", b)], [("qT", qi)])

    def b_tiles(i):
        if i in (0, 1):
            js = [0, 1, 2, 3]
            return [(j, ("s", m)) for m, j in enumerate(js)]
        if i in (NBLK - 2, NBLK - 1):
            js = [NBLK - 4, NBLK - 3, NBLK - 2, NBLK - 1]
            return [(j, ("s", m)) for m, j in enumerate(js)]
        return [(i + o, ("i", o + 2)) for o in range(-2, 3)]

    bmask_ctr = [0]
    special_ids = {0: 0, 1: 1, NBLK - 2: 2, NBLK - 1: 3}
    cur_special = [0]

    def ensure_special(i):
        sid = special_ids[i]
        if cur_special[0] != sid:
            mflat = maskBs.rearrange("p a b c -> p (a b c)")
            for c0 in range(0, 8 * 4 * 128, 1024):
                xi = rr_xp.next()
                dma_load(xp[xi], biasBs_d[sid][:, c0:c0 + 1024], f"ld_x{xi}", [("xp", xi)])
                P.op("act", lambda e, o=mflat[:, c0:c0 + 1024], i_=xp[xi]: e.activation(o, i_, AF.Exp),
                     [("xp", xi)], ["maskBs"])
            cur_special[0] = sid

    YBA = [0, 1]
    YBB = [2, 3]
    rr_sb = RR(2)
    rr_db = RR(2)
    SBK = [4, 5]
    DBK = [6, 7]

    def gen_attn(s):
        ytile = yT[s % 2]
        tasks = []
        for blk in range(NB1):
            i = s * NB1 + blk
            ja = [j for j in (i - 1, i, i + 1) if 0 <= j < NBLK]
            for g in range(2):
                for j in ja:
                    tasks.append(dict(kind="A", blk=blk, i=i, g=g, j=j, msel=j - i + 1,
                                      first=(j == ja[0]), last=(j == ja[-1]), grp_first=(g == 0 and j == ja[0]),
                                      grp_last=(g == 1 and j == ja[-1])))
            bt = b_tiles(i)
            for n, (j, msel) in enumerate(bt):
                for hbk in range(2):
                    tasks.append(dict(kind="B", blk=blk, i=i, g=hbk, j=j, msel=msel,
                                      first=(n == 0), last=(n == len(bt) - 1), grp_first=(n == 0 and hbk == 0),
                                      grp_last=(n == len(bt) - 1 and hbk == 1)))

        def emit_qk(t):
            g, j, blk = t["g"], t["j"], t["blk"]
            tq = slice(blk * 128, (blk + 1) * 128)
            rs = (j // NB1) % 3
            pos = j % NB1
            kcols = slice(pos * 128, (pos + 1) * 128)
            b = SBK[rr_sb.next()]
            ps = bank_f32(b)
            if t["kind"] == "A":
                pr = slice(64 * g, 64 * g + 64)
                P.op("pe", lambda e, o=ps.rearrange("p (a b) -> p a b", a=4), l=kT[rs][pr, 0, kcols], r=qT[pr, 0:4, tq]:
                     e.matmul(o, l, r, start=True, stop=True),
                     [("kT", rs, 0)] + [("qT", c) for c in range(4)], [("bank", b)])
            else:
                for hh in range(4):
                    h = 2 * hh + g
                    pr = slice(64 * (h % 2), 64 * (h % 2) + 64)
                    P.op("pe", lambda e, o=ps[:, hh * 128:(hh + 1) * 128], l=kT[rs][pr, 1 + h // 2, kcols], r=qT[pr, 4 + h // 2, tq]:
                         e.matmul(o, l, r, start=True, stop=True),
                         [("kT", rs, 1 + h // 2), ("qT", 4 + h // 2)], [("bank", b)])
            return b

        def emit_soft(t, b):
            g, msel = t["g"], t["msel"]
            ps = bank_f32(b)
            ui = rr_pu.next()
            ti = rr_pt.next()
            P.op("act", lambda e, o=pu[ui], i_=ps: e.activation(o, i_, AF.Exp), [("bank", b)], [("pu", ui)])
            if t["kind"] == "A":
                m_ap = maskA[:, 4 * g:4 * g + 4, msel, :]
                mkey = "maskA"
                eng = "dve"
            else:
                if msel[0] == "i":
                    m_ap = maskBi[:, g:8:2, msel[1], :]
                    mkey = "maskBi"
                else:
                    m_ap = maskBs[:, g:8:2, msel[1], :]
                    mkey = "maskBs"
                eng = "pool" if (bmask_ctr[0] % 5) != 4 else "dve"
                bmask_ctr[0] += 1
            P.op(eng, lambda e, o=pt[ti], a=pu[ui].rearrange("p (a b) -> p a b", a=4), m=m_ap:
                 e.tensor_tensor(o, a, m, ALU.mult), [("pu", ui), mkey], [("pt", ti)])
            return ti

        def ybanks(kind):
            return YBA if kind == "A" else YBB

        def emit_pv(t, ti):
            g, j = t["g"], t["j"]
            rs = (j // NB1) % 3
            pos = j % NB1
            yb = ybanks(t["kind"])[g]
            yps = bank_f32(yb)
            for c in range(4):
                vh = g if t["kind"] == "A" else 2 + 2 * c + g
                P.op("pe", lambda e, o=yps[:, c * 65:(c + 1) * 65], l=pt[ti][:, c, :], r=vR[rs][:, pos, vh, :]:
                     e.matmul(o, l, r, start=False, stop=t["last"], skip_group_check=True),
                     [("pt", ti), ("v", rs, pos)], [("bank", yb)])

        def emit_zero(kind):
            for yb_ in ybanks(kind):
                P.op("pe", lambda e, o=bank_f32(yb_)[:, 0:260]: e.matmul(o, zt[:, 0:128], zt[:, 0:260], start=True, stop=False,
                                                                          skip_group_check=True), ["zt"], [("bank", yb_)])

        def emit_norm(kind, ybuf):
            for g in range(2):
                yb = ybanks(kind)[g]
                yps = bank_f32(yb)[:, 0:260].rearrange("p (h d) -> p h d", h=4)
                di = rr_den.next()
                dn = den[:, di, :]
                if kind == "A":
                    P.op("dve", lambda e, o=dn, a=yps[:, :, 64], b_=es[:, 4 * g:4 * g + 4]:
                         e.tensor_tensor(o, a, b_, ALU.add), [("bank", yb), "es"], [("den", di)])
                    P.op("dve", lambda e, o=dn: e.reciprocal(o, o), [("den", di)], [("den", di)])
                    yo = ysb[ybuf][:, g * 256:(g + 1) * 256].rearrange("p (h d) -> p h d", h=4)
                else:
                    P.op("dve", lambda e, o=dn, a=yps[:, :, 64]: e.reciprocal(o, a), [("bank", yb)], [("den", di)])
                    yo = ysb[ybuf][:, 512:1024].rearrange("p (h d) -> p h d", h=8)[:, g:8:2, :]
                P.op("dve", lambda e, o=yo, a=yps[:, :, 0:64], r=dn:
                     e.tensor_tensor(o, a, r.rearrange("p (h o) -> p h o", o=1).broadcast_to([128, 4, 64]), ALU.mult),
                     [("bank", yb), ("den", di)], [("ysb", ybuf, kind)])

        def emit_ytrans(blk, i):
            ybuf = blk
            tq = slice(blk * 128, (blk + 1) * 128)
            if debug:
                P.dma("pool", f"dbg_y{ybuf}", lambda e, o=ydbg_d[i * 128:(i + 1) * 128, :], i_=ysb[ybuf]: e.dma_start(out=o, in_=i_),
                      reads=[("ysb", ybuf, "A"), ("ysb", ybuf, "B")], writes=[("ydbg", i)])
            b = SBK[rr_sb.next()]
            pst = bank_bf16(b)
            for fc in range(8):
                P.op("pe", lambda e, o=pst[:, fc * 128:(fc + 1) * 128], i_=ysb[ybuf][:, fc * 128:(fc + 1) * 128]:
                     e.transpose(o, i_, ident), [("ysb", ybuf, "A"), ("ysb", ybuf, "B"), "ident"], [("bank", b)])
            evac_copy("act", ytile[:, :, tq], pst.rearrange("p (a b) -> p a b", a=8), [("bank", b)], [("yT", s % 2, blk)])

        def retire(t, ti):
            if t["grp_first"]:
                emit_zero(t["kind"])
            emit_pv(t, ti)
            if t["grp_last"]:
                emit_norm(t["kind"], t["blk"])
                if t["kind"] == "B":
                    emit_ytrans(t["blk"], t["i"])

        LOOK = 4
        pend = []
        for t in tasks:
            if t["kind"] == "B" and t["grp_first"] and t["i"] in special_ids:
                ensure_special(t["i"])
            b = emit_qk(t)
            ti = emit_soft(t, b)
            pend.append((t, ti))
            yield
            if len(pend) > LOOK:
                retire(*pend.pop(0))
        while pend:
            retire(*pend.pop(0))
            yield

    def gen_proj(s):
        slot = s % 3
        ytile = yT[s % 2]
        yT_keys = [("yT", s % 2, b) for b in range(NB1)]
        for c in range(8):
            bg = DBK[rr_db.next()]
            psg = bank_f32(bg)
            for half, goff in enumerate((GA, GB)):
                for kc in range(KC):
                    P.op("pe", lambda e, o=psg[:, half * T1:(half + 1) * T1], l=win[:, kc, goff + c * 128:goff + (c + 1) * 128],
                         r=hT[slot][:, kc, :], kc=kc: e.matmul(o, l, r, start=(kc == 0), stop=(kc == KC - 1)),
                         win_keys(goff + c * 128, goff + (c + 1) * 128) + hT_keys(slot), [("bank", bg)])
                yield
            gi = rr_alt.next()
            for half in range(2):
                P.op("act", lambda e, o=sg[gi][:, half * T1:(half + 1) * T1], i_=psg[:, half * T1:(half + 1) * T1],
                     bb=bgate[:, half * 8 + c:half * 8 + c + 1]: e.activation(o, i_, AF.Sigmoid, bias=bb),
                     [("bank", bg), "bgate"], [("sg", gi)])
            bu = DBK[rr_db.next()]
            psu = bank_f32(bu)
            for half, wp in enumerate((wpa, wpb)):
                for fc in range(4):
                    P.op("pe", lambda e, o=psu[:, half * T1:(half + 1) * T1], l=wp[:, fc, c * 128:(c + 1) * 128],
                         r=ytile[:, half * 4 + fc, :], fc=fc: e.matmul(o, l, r, start=(fc == 0), stop=(fc == 3)),
                         ["wpa", "wpb"] + yT_keys, [("bank", bu)])
                if half == 0:
                    yield
            P.op("dve", lambda e, o=sg[gi], a=sg[gi], u=psu: e.tensor_tensor(o, a, u, ALU.mult),
                 [("sg", gi), ("bank", bu)], [("sg", gi)])
            P.op("pool", lambda e, o=mT[:, c, :], a=sg[gi][:, 0:T1], b_=sg[gi][:, T1:2 * T1]:
                 e.tensor_tensor(o, a, b_, ALU.add), [("sg", gi)], [("mT", c)])
            yield

    def gen_out(s):
        mT_keys = [("mT", c) for c in range(8)]
        if debug:
            P.dma("pool", "dbg_m", lambda e, o=mdbg_d.rearrange("(c p) t -> p c t", p=128)[:, :, s * T1:(s + 1) * T1], i_=mT:
                  e.dma_start(out=o, in_=i_), reads=mT_keys, writes=[("mdbg", s)])
        for blk in range(NB1):
            r0 = s * T1 + blk * 128
            xi = rr_xp.next()
            dma_load(xp[xi], x_d[r0:r0 + 128, :], f"ld_x{xi}", [("xp", xi)])
            for half in range(2):
                b = DBK[rr_db.next()]
                ps = bank_f32(b)
                for c in range(8):
                    P.op("pe", lambda e, o=ps, l=mT[:, c, blk * 128:(blk + 1) * 128], r=wout[:, c, half * 512:(half + 1) * 512], c=c:
                         e.matmul(o, l, r, start=(c == 0), stop=(c == 7)), mT_keys + ["wout"], [("bank", b)])
                    if c == 3:
                        yield
                P.op("dve", lambda e, o=xp[xi][:, half * 512:(half + 1) * 512], a=ps:
                     e.tensor_tensor(o, a, o, ALU.add), [("bank", b), ("xp", xi)], [("xp", xi)])
                yield
            P.dma("sp", f"st_x{xi}", lambda e, o=x1_d[r0:r0 + 128, :], i_=xp[xi]: e.dma_start(out=o, in_=i_),
                  reads=[("xp", xi)], writes=[("x1", r0 // 128)])

    def gen_chain(*gens):
        for g_ in gens:
            yield from g_

    def interleave(ga, gd):
        a_done = d_done = False
        while not (a_done and d_done):
            if not a_done:
                try:
                    next(ga)
                except StopIteration:
                    a_done = True
            if not d_done:
                try:
                    next(gd)
                except StopIteration:
                    d_done = True

    stage_prep1(0)
    stage_trans1(0)
    stage_prep1(1)
    stage_kv(0)
    for s in range(NS1):
        if s + 1 < NS1:
            stage_trans1(s + 1)
            if s + 2 < NS1:
                stage_prep1(s + 2)
            stage_kv(s + 1)
        stage_q(s)
        dense = gen_chain(gen_proj(s - 1), gen_out(s - 1)) if s >= 1 else iter(())
        interleave(gen_attn(s), dense)
    interleave(iter(()), gen_chain(gen_proj(NS1 - 1), gen_out(NS1 - 1)))
    if stop_after is not None:
        P.wait_all("sp")
        P.replay(nc)
        return nc

    P.barrier()
    AR.reset(base_mark)
    wup = AR.alloc(BF16, [KC, DFF])
    wdn = AR.alloc(BF16, [32, D])
    NXB = 5
    xb = [AR.alloc(F32, [D]) for _ in range(NXB)]
    NHB2 = 4
    hb2 = [AR.alloc(BF16, [D]) for _ in range(NHB2)]
    junk2 = AR.alloc(BF16, [D])
    h2T = AR.alloc(BF16, [KC, T2])
    aT = AR.alloc(BF16, [32, T2])
    NRT = 3
    rt = [AR.alloc(BF16, [T2]) for _ in range(NRT)]
    gfin = AR.alloc(F32, [D])
    rr_xb = RR(NXB)
    rr_bank = RR(8)
    rr_rt = RR(NRT)
    rr_hb2 = RR(NHB2)

    dma_load(gfin, gfin_d.partition_broadcast(128), "c_gfin", ["gfin"])
    wup_v = wup_d.rearrange("(kc p) n -> p kc n", p=128)
    for cb in range(4):
        c0 = cb * 1024
        for kc in range(KC):
            xi = rr_xb.next()
            dma_load(xb[xi], wup_v[:, kc, c0:c0 + 1024], f"ld_b{xi}", [("xb", xi)])
            eng = "dve" if rr_alt.next() == 0 else "act"
            P.op(eng, fold_scale(eng, wup[:, kc, c0:c0 + 1024], xb[xi], gmlp[:, kc:kc + 1]),
                 [("xb", xi), "gmlp"], [("wup", cb)])

    for fc in range(32):
        P.dma("pool", f"w_d{fc // 4}", lambda e, fc=fc: e.dma_start(out=wdn[:, fc, :], in_=wdn_d[fc * 128:(fc + 1) * 128, :]), reads=[("wup", 2)], writes=[("wdn", fc // 4)])
    hb2keys = "hb2"

    def prep2(u, blk):
        r0 = u * T2 + blk * 128
        xi = rr_xb.next()
        dma_load(xb[xi], x1_d[r0:r0 + 128, :], f"ld_b{xi}", [("xb", xi)], rkeys=[("x1", r0 // 128)])
        norm_prep2(xb[xi], ("xb", xi), ("p2", u, blk))
        return xi

    def norm_prep2(x_ap, x_key, tag):
        hi = rr_hb2.next()
        si = rr_ss.next()
        P.op("act", lambda e, o=hb2[hi], i_=x_ap, a=ss[:, si:si + 1]: e.activation(o, i_, AF.Square, accum_out=a),
             [x_key], [("hb2", hi), ("ss", si)])
        emit_rstd(si)
        P.op("act", lambda e, o=hb2[hi], i_=x_ap, m=rstd[:, si:si + 1]: e.mul(o, i_, m),
             [x_key, ("rstd", si)], [("hb2", hi)])
        prep_state[tag] = hi

    def trans2(u, blk):
        hi = prep_state.pop(("p2", u, blk))
        b = rr_bank.next()
        pst = bank_bf16(b)
        for kc in range(KC):
            P.op("pe", lambda e, o=pst[:, kc * 128:(kc + 1) * 128], i_=hb2[hi][:, kc * 128:(kc + 1) * 128]:
                 e.transpose(o, i_, ident), [("hb2", hi), "ident"], [("bank", b)])
        eng = "dve" if rr_alt.next() == 0 else "act"
        evac_copy(eng, h2T[:, :, blk * 128:(blk + 1) * 128], pst.rearrange("p (a b) -> p a b", a=KC),
                  [("bank", b)], [("h2T", blk)])

    h2T_keys = [("h2T", b) for b in range(NB2)]
    xis = {}
    for blk in range(NB2):
        xis[(0, blk)] = prep2(0, blk)
    for blk in range(NB2):
        trans2(0, blk)
    for u in range(NS2):
        for fc in range(32):
            b = rr_bank.next()
            ps = bank_f32(b)
            for kc in range(KC):
                P.op("pe", lambda e, o=ps, l=wup[:, kc, fc * 128:(fc + 1) * 128], r=h2T[:, kc, :], kc=kc:
                     e.matmul(o, l, r, start=(kc == 0), stop=(kc == KC - 1)), [("wup", fc // 8)] + h2T_keys, [("bank", b)])
            ri = rr_rt.next()
            P.op("act", lambda e, o=rt[ri], i_=ps: e.activation(o, i_, AF.Relu), [("bank", b)], [("rt", ri)])
            P.op("dve", lambda e, o=aT[:, fc, :], a=rt[ri]: e.tensor_tensor(o, a, a, ALU.mult), [("rt", ri)], [("aT", fc)])
        if u + 1 < NS2:
            xis[(u + 1, 0)] = prep2(u + 1, 0)
        for blk in range(NB2):
            xi = xis[(u, blk)]
            r0 = u * T2 + blk * 128
            for half in range(2):
                b = rr_bank.next()
                ps = bank_f32(b)
                for fc in range(32):
                    P.op("pe", lambda e, o=ps, l=aT[:, fc, blk * 128:(blk + 1) * 128], r=wdn[:, fc, half * 512:(half + 1) * 512], fc=fc:
                         e.matmul(o, l, r, start=(fc == 0), stop=(fc == 31)), [("aT", fc), ("wdn", fc // 4)], [("bank", b)])
                P.op("dve", lambda e, o=xb[xi][:, half * 512:(half + 1) * 512], a=ps:
                     e.tensor_tensor(o, a, o, ALU.add), [("bank", b), ("xb", xi)], [("xb", xi)])
            si = rr_ss.next()
            P.op("act", lambda e, o=junk2, i_=xb[xi], a=ss[:, si:si + 1]: e.activation(o, i_, AF.Square, accum_out=a),
                 [("xb", xi)], ["junk2", ("ss", si)])
            emit_rstd(si)
            P.op("dve", lambda e, o=xb[xi], m=rstd[:, si:si + 1]:
                 e.scalar_tensor_tensor(o, o, m, gfin, ALU.mult, ALU.mult), [("xb", xi), ("rstd", si), "gfin"], [("xb", xi)])
            P.dma("sp", f"st_b{xi}", lambda e, o=out_d[r0:r0 + 128, :], i_=xb[xi]: e.dma_start(out=o, in_=i_),
                  reads=[("xb", xi)], writes=[("out", r0 // 128)])
            if u + 1 < NS2 and blk + 1 < NB2:
                xis[(u + 1, blk + 1)] = prep2(u + 1, blk + 1)
        if u + 1 < NS2:
            for blk in range(NB2):
                trans2(u + 1, blk)
    P.wait_all("sp")
    P.replay(nc)
    return nc


def _perm_cols():
    cols = []
    for c in range(4):
        cols += list(range(c * 64, (c + 1) * 64))
        cols += list(range((4 + c) * 64, (5 + c) * 64))
    cols += list(range(512, 640))
    cols += list(range(768, 1280))
    cols += list(range(1280, 1792))
    cols += list(range(640, 768))
    cols += list(range(1792, 2304))
    cols += list(range(2304, 4352))
    return np.array(cols)


def _bias_a():
    k = np.arange(128)[:, None]
    q = np.arange(128)[None, :]
    out = np.full((128, 8, 3, 128), NEG, np.float32)
    for h in range(8):
        slope = 2.0 ** (-(h + 1))
        for sl in range(3):
            dist = np.abs((sl - 1) * 128 + k - q)
            out[:, h, sl, :] = np.where(dist <= 128, -slope * dist, NEG)
    return out.reshape(128, -1)


def _bias_b_gather(rpb, rows, i, js):
    out = np.full((128, 8, len(js), 128), NEG, np.float32)
    kk = np.arange(128)
    a, ck = kk // 64, kk % 64
    b, cq = kk // 64, kk % 64
    cs = np.clip(cq - 8, 0, 48)
    r = 2 * i + b
    r0 = np.clip(r - 4, 0, rows - 8)
    for m, j in enumerate(js):
        R = 2 * j + a
        vr = (R[:, None] >= r0[None, :]) & (R[:, None] <= r0[None, :] + 7)
        vc = (ck[:, None] >= cs[None, :]) & (ck[:, None] <= cs[None, :] + 15)
        valid = vr & vc
        dr = np.clip(R[:, None] - r[None, :] + 7, 0, 14)
        dc = np.clip(ck[:, None] - cq[None, :] + 15, 0, 30)
        g = rpb[:, dr, dc]
        g = np.where(valid[None], g, np.float32(NEG))
        out[:, :, m, :] = np.transpose(g, (1, 0, 2))
    return out


_NC_CACHE = {}


def _get_nc(S, debug=False):
    key = (S, debug)
    if key not in _NC_CACHE:
        _NC_CACHE[key] = build_program(S, debug)
    return _NC_CACHE[key]


def make_in_maps(x, norm_mix, w_in, b_gate, sink, rpb, w_proj_a, w_proj_b, w_out,
                 norm_mlp, w_up, w_down, norm_final):
    f = lambda a: np.ascontiguousarray(np.asarray(a, dtype=np.float32))
    x = f(x)
    B, S, _ = x.shape
    rows = S // 64
    NBLK = S // 128
    rpb0 = f(rpb)[0]
    common = {
        "w_in": f(f(w_in)[0][:, _perm_cols()]),
        "w_proj_a": f(w_proj_a)[0], "w_proj_b": f(w_proj_b)[0], "w_out": f(w_out)[0],
        "w_up": f(w_up)[0], "w_down": f(w_down)[0],
        "gmix": f(f(norm_mix)[0].reshape(KC, 128).T),
        "gmlp": f(f(norm_mlp)[0].reshape(KC, 128).T),
        "bgate": f(f(b_gate)[0].reshape(16, 128).T),
        "sink": f(sink)[0], "gfin": f(norm_final),
        "ident": np.eye(128, dtype=np.float32),
        "biasA": _bias_a(),
        "biasBi": f(_bias_b_gather(rpb0, rows, 2, [0, 1, 2, 3, 4]).reshape(128, -1)),
        "biasBs": f(np.stack([
            _bias_b_gather(rpb0, rows, 0, [0, 1, 2, 3]).reshape(128, -1),
            _bias_b_gather(rpb0, rows, 1, [0, 1, 2, 3]).reshape(128, -1),
            _bias_b_gather(rpb0, rows, NBLK - 2, [NBLK - 4, NBLK - 3, NBLK - 2, NBLK - 1]).reshape(128, -1),
            _bias_b_gather(rpb0, rows, NBLK - 1, [NBLK - 4, NBLK - 3, NBLK - 2, NBLK - 1]).reshape(128, -1),
        ])),
    }
    return [dict(common, x=f(x[b])) for b in range(B)]


def kernel(x, norm_mix, w_in, b_gate, sink, rpb, w_proj_a, w_proj_b, w_out,
           norm_mlp, w_up, w_down, norm_final):
    in_maps = make_in_maps(x, norm_mix, w_in, b_gate, sink, rpb, w_proj_a, w_proj_b, w_out,
                           norm_mlp, w_up, w_down, norm_final)
    B = len(in_maps)
    S = in_maps[0]["x"].shape[0]
    nc = _get_nc(S)
    res = run_bass_kernel_spmd(nc, in_maps, core_ids=list(range(B)))
    return np.stack([np.asarray(r["out"], dtype=np.float32) for r in res.results], axis=0)
```

```python
import os
import numpy as np
import concourse.bass as bass
import concourse.mybir as mybir
from concourse.bass_utils import run_bass_kernel_spmd

F32 = mybir.dt.float32
BF16 = mybir.dt.bfloat16
AF = mybir.ActivationFunctionType
ALU = mybir.AluOpType

D = 1024
KC = 8
NIN = 4352
DFF = 4096
QA, KA, QB, KB, VA, VB, GA, GB = 0, 512, 640, 1152, 1664, 1792, 2304, 3328
NEG = -30000.0
EPS = 1e-6
T1 = 256
T2 = 512
ENGS = ("pe", "act", "dve", "pool", "sp")


class Prog:
    def __init__(self, same_engine_sync=True):
        self.ops = {e: [] for e in ENGS}
        self.ncomp = {e: 0 for e in ENGS}
        self.chan_cnt = {}
        self.lastw = {}
        self.readers = {}
        self.waited = {e: {} for e in ENGS}
        self.signals = set()
        self.same_engine_sync = same_engine_sync

    def _deps(self, reads, writes):
        deps = {}
        for r in reads:
            w = self.lastw.get(r)
            if w is not None:
                deps[w] = True
        for w_ in writes:
            w = self.lastw.get(w_)
            if w is not None:
                deps.setdefault(w, False)
            for c, i in self.readers.get(w_, {}).items():
                deps.setdefault((c, i), False)
        return deps

    def _waits(self, eng, deps):
        if not isinstance(deps, dict):
            deps = {d_: True for d_ in deps}
        best = {}
        for (c, i), raw in deps.items():
            if c == eng and (eng == "pe" or not self.same_engine_sync or not raw):
                continue
            if best.get(c, 0) < i:
                best[c] = i
        waits = []
        for c, i in best.items():
            if self.waited[eng].get(c, 0) >= i:
                continue
            self.waited[eng][c] = i
            waits.append((c, i))
            if not c.startswith("dma:"):
                self.signals.add((c, i))
        return waits

    def _commit(self, counter, idx, reads, writes):
        for r in reads:
            self.readers.setdefault(r, {})[counter] = idx
        for w_ in writes:
            self.lastw[w_] = (counter, idx)
            self.readers[w_] = {}

    def op(self, eng, fn, reads=(), writes=()):
        waits = self._waits(eng, self._deps(reads, writes))
        self.ncomp[eng] += 1
        idx = self.ncomp[eng]
        self.ops[eng].append(dict(fn=fn, waits=waits, kind="c", idx=idx))
        self._commit(eng, idx, reads, writes)

    def dma(self, queue, chan, fn, reads=(), writes=()):
        waits = self._waits(queue, self._deps(reads, writes))
        c = "dma:" + chan
        self.chan_cnt[c] = self.chan_cnt.get(c, 0) + 1
        idx = self.chan_cnt[c]
        self.ops[queue].append(dict(fn=fn, waits=waits, kind="d", chan=c))
        self._commit(c, idx, reads, writes)

    def wait_all(self, eng, include_dma=True):
        deps = set()
        for e in ENGS:
            if e != eng and self.ncomp[e] > 0:
                deps.add((e, self.ncomp[e]))
        if include_dma:
            for c, n in self.chan_cnt.items():
                deps.add((c, n))
        waits = self._waits(eng, deps)
        self.ops[eng].append(dict(fn=None, waits=waits, kind="w"))

    def barrier(self):
        snap = {e: self.ncomp[e] for e in ENGS}
        chans = dict(self.chan_cnt)
        for eng in ENGS:
            deps = set((e, n) for e, n in snap.items() if e != eng and n > 0)
            deps |= set(chans.items())
            waits = self._waits(eng, deps)
            self.ops[eng].append(dict(fn=None, waits=waits, kind="w"))

    def replay(self, nc):
        chan_names = sorted(self.chan_cnt.keys())
        semval = {}
        for e in ENGS:
            c = 0
            for i in range(1, self.ncomp[e] + 1):
                if (e, i) in self.signals:
                    c += 1
                semval[(e, i)] = c
        import contextlib
        with contextlib.ExitStack() as st:
            sems = {}
            for e in ENGS:
                sems[e] = st.enter_context(nc.semaphore("s_" + e))
            for c in chan_names:
                sems[c] = st.enter_context(nc.semaphore("s_" + c.replace(":", "_")))
            block = st.enter_context(nc.Block())

            def run(engname, engobj):
                for o in self.ops[engname]:
                    for (c, i) in o["waits"]:
                        if c.startswith("dma:"):
                            engobj.wait_ge(sems[c], 16 * i)
                        else:
                            engobj.wait_ge(sems[c], semval[(c, i)])
                    if o["fn"] is None:
                        continue
                    ins = o["fn"](engobj)
                    if o["kind"] == "d":
                        ins.then_inc(sems[o["chan"]], 16)
                    elif (engname, o["idx"]) in self.signals:
                        ins.then_inc(sems[engname], 1)

            @block.tensor
            def _(e):
                run("pe", e)

            @block.scalar
            def _(e):
                run("act", e)

            @block.vector
            def _(e):
                run("dve", e)

            @block.gpsimd
            def _(e):
                run("pool", e)

            @block.sync
            def _(e):
                run("sp", e)


class Arena:
    def __init__(self, handle, nfloats):
        self.h = handle
        self.n = nfloats
        self.off = 0

    def mark(self):
        return self.off

    def reset(self, m):
        self.off = m

    def alloc(self, dtype, shape):
        nel = int(np.prod(shape))
        nbytes = nel * (2 if dtype == BF16 else 4)
        nfl = (nbytes + 3) // 4
        nfl = (nfl + 7) // 8 * 8
        assert self.off + nfl <= self.n, f"SBUF arena overflow: need {self.off + nfl} > {self.n}"
        ap = self.h[:, self.off:self.off + nfl]
        self.off += nfl
        if dtype == BF16:
            ap = ap.bitcast(BF16)
        ap = ap[:, 0:nel]
        if len(shape) == 2:
            ap = ap.rearrange("p (a b) -> p a b", a=shape[0])
        elif len(shape) == 3:
            ap = ap.rearrange("p (a b c) -> p a b c", a=shape[0], b=shape[1])
        elif len(shape) == 4:
            ap = ap.rearrange("p (a b c d) -> p a b c d", a=shape[0], b=shape[1], c=shape[2])
        return ap


class _Stop(Exception):
    pass


def build_program(S, debug=False, stop_after=None):
    assert S % T2 == 0 and S >= 1024
    NBLK = S // 128
    NS1 = S // T1
    NS2 = S // T2
    NB1 = T1 // 128
    NB2 = T2 // 128

    nc = bass.Bass("TRN2", target_bir_lowering=False)

    def din(name, shape):
        return nc.dram_tensor(name, list(shape), F32, kind="ExternalInput").ap()

    x_d = din("x", [S, D])
    win_d = din("w_in", [D, NIN])
    wpa_d = din("w_proj_a", [512, D])
    wpb_d = din("w_proj_b", [512, D])
    wout_d = din("w_out", [D, D])
    wup_d = din("w_up", [D, DFF])
    wdn_d = din("w_down", [DFF, D])
    gmix_d = din("gmix", [128, KC])
    gmlp_d = din("gmlp", [128, KC])
    bgate_d = din("bgate", [128, 16])
    sink_d = din("sink", [8])
    gfin_d = din("gfin", [D])
    ident_d = din("ident", [128, 128])
    biasA_d = din("biasA", [128, 8 * 3 * 128])
    biasBi_d = din("biasBi", [128, 8 * 5 * 128])
    biasBs_d = din("biasBs", [4, 128, 8 * 4 * 128])
    out_d = nc.dram_tensor("out", [S, D], F32, kind="ExternalOutput").ap()
    x1_d = nc.dram_tensor("x1s", [S, D], F32, kind="ExternalOutput" if debug else "Internal").ap()

    if debug:
        ydbg_d = nc.dram_tensor("ydbg", [S, D], F32, kind="ExternalOutput").ap()
        mdbg_d = nc.dram_tensor("mdbg", [D, S], F32, kind="ExternalOutput").ap()
    P = Prog()

    NFL = 212800 // 4
    arena_h = nc.alloc_sbuf_tensor("arena", [128, NFL], F32)
    AR = Arena(arena_h, NFL)
    banks = [nc.alloc_psum_tensor(f"ps{i}", [128, 512], F32) for i in range(8)]

    def bank_f32(b):
        return banks[b][:, :]

    def bank_bf16(b):
        return banks[b][:, :].bitcast(BF16)

    ident = AR.alloc(BF16, [128])
    gmix = AR.alloc(F32, [KC])
    gmlp = AR.alloc(F32, [KC])
    bgate = AR.alloc(F32, [16])
    nbgate = AR.alloc(F32, [16])
    ones1 = AR.alloc(F32, [8])
    es = AR.alloc(F32, [8])
    sinkb = AR.alloc(F32, [8])
    ss = AR.alloc(F32, [16])
    rstd = AR.alloc(F32, [16])
    den = AR.alloc(F32, [8, 4])
    base_mark = AR.mark()

    win = AR.alloc(BF16, [KC, NIN])
    wpa = AR.alloc(BF16, [4, D])
    wpb = AR.alloc(BF16, [4, D])
    wout = AR.alloc(BF16, [KC, D])
    hT = [AR.alloc(BF16, [KC, T1]) for _ in range(3)]
    kT = [AR.alloc(BF16, [5, T1]) for _ in range(3)]
    vR = [AR.alloc(BF16, [NB1, 10, 65]) for _ in range(3)]
    qT = AR.alloc(BF16, [8, T1])
    maskA = AR.alloc(BF16, [8, 3, 128])
    maskBi = AR.alloc(BF16, [8, 5, 128])
    maskBs = AR.alloc(BF16, [8, 4, 128])
    NPU, NPT = 4, 6
    pu = [AR.alloc(BF16, [512]) for _ in range(NPU)]
    pt = [AR.alloc(BF16, [4, 128]) for _ in range(NPT)]
    ysb = [AR.alloc(BF16, [D]) for _ in range(2)]
    yT = [AR.alloc(BF16, [8, T1]) for _ in range(2)]
    sg = [AR.alloc(F32, [512]) for _ in range(2)]
    mT = AR.alloc(BF16, [KC, T1])
    NXP = 4
    xp = [AR.alloc(F32, [D]) for _ in range(NXP)]
    hb = [AR.alloc(BF16, [D]) for _ in range(2)]
    zt = AR.alloc(BF16, [512])

    class RR:
        def __init__(self, n):
            self.n, self.i = n, 0

        def next(self):
            v = self.i % self.n
            self.i += 1
            return v

    rr_xp = RR(NXP)
    rr_hb = RR(2)
    rr_ss = RR(16)
    class RRB:
        def __init__(self):
            self.i = 0

        def next(self):
            v = self.i % 8
            self.i += 1
            return v

    rr_bank = RRB()
    rr_pu = RR(NPU)
    rr_pt = RR(NPT)
    rr_den = RR(8)
    rr_alt = RR(2)

    def emit_rstd(si):
        r = rstd[:, si:si + 1]
        P.op("dve", lambda e, o=r, i=ss[:, si:si + 1]: e.tensor_scalar(o, i, 1.0 / D, EPS, ALU.mult, ALU.add),
             [("ss", si)], [("rstd", si)])
        P.op("act", lambda e, o=r: e.activation(o, o, AF.Ln), [("rstd", si)], [("rstd", si)])
        P.op("act", lambda e, o=r: e.activation(o, o, AF.Exp, scale=-0.5), [("rstd", si)], [("rstd", si)])

    def dma_load(buf_ap, src_ap, chan, wkeys, rkeys=(), queue="sp"):
        P.dma(queue, chan, lambda e, o=buf_ap, i=src_ap: e.dma_start(out=o, in_=i), reads=rkeys, writes=wkeys)

    def evac_copy(eng, out_ap, in_ap, rkeys, wkeys, mul=None):
        if eng == "act":
            if mul is None:
                P.op("act", lambda e, o=out_ap, i=in_ap: e.copy(o, i), rkeys, wkeys)
            else:
                P.op("act", lambda e, o=out_ap, i=in_ap, m=mul: e.mul(o, i, m), rkeys, wkeys)
        else:
            if mul is None:
                P.op(eng, lambda e, o=out_ap, i=in_ap: e.tensor_copy(o, i), rkeys, wkeys)
            else:
                P.op(eng, lambda e, o=out_ap, i=in_ap, m=mul: e.tensor_scalar(o, i, m, None, ALU.mult), rkeys, wkeys)

    dma_load(gmix, gmix_d, "c_gmix", ["gmix"])
    dma_load(gmlp, gmlp_d, "c_gmlp", ["gmlp"])
    dma_load(bgate, bgate_d, "c_bgate", ["bgate"])
    dma_load(sinkb, sink_d.partition_broadcast(128), "c_sink", ["sinkb"])
    P.dma("pool", "c_ident", lambda e: e.dma_start(out=ident, in_=ident_d), writes=["ident"])
    P.op("act", lambda e: e.activation(es, sinkb, AF.Exp), ["sinkb"], ["es"])
    P.op("dve", lambda e: e.tensor_scalar(nbgate, bgate, -1.0, None, ALU.mult), ["bgate"], ["nbgate"])
    P.op("dve", lambda e: e.memset(ones1, 1.0), [], ["ones1"])
    P.op("pool", lambda e: e.memset(zt, 0.0), [], ["zt"])
    for r in range(3):
        P.op("pool", lambda e, r=r: e.memset(vR[r].rearrange("p a b c -> p (a b c)"), 1.0), [], [("v", r, b) for b in range(NB1)])

    def load_mask(mask_ap, src_ap, ncols, key):
        mflat = mask_ap.rearrange("p a b c -> p (a b c)")
        for c0 in range(0, ncols, 1024):
            xi = rr_xp.next()
            dma_load(xp[xi], src_ap[:, c0:c0 + 1024], f"ld_x{xi}", [("xp", xi)])
            P.op("act", lambda e, o=mflat[:, c0:c0 + 1024], i=xp[xi]: e.activation(o, i, AF.Exp),
                 [("xp", xi)], [key])

    load_mask(maskA, biasA_d, 8 * 3 * 128, "maskA")
    load_mask(maskBi, biasBi_d, 8 * 5 * 128, "maskBi")
    load_mask(maskBs, biasBs_d[0], 8 * 4 * 128, "maskBs")

    win_v = win_d.rearrange("(kc p) n -> p kc n", p=128)
    pieces = [(0, 1024), (1024, 1024), (2048, 1024), (3072, 1024), (4096, 256)]

    def fold_scale(eng, o, i_, g):
        if eng == "act":
            P_fn = lambda e, o=o, i_=i_, g=g: e.mul(o, i_, g)
        else:
            P_fn = lambda e, o=o, i_=i_, g=g: e.tensor_scalar(o, i_, g, None, ALU.mult)
        return P_fn

    for cb in (1, 0, 2, 3, 4):
        c0, w = pieces[cb]
        for kc in range(KC):
            xi = rr_xp.next()
            dma_load(xp[xi][:, 0:w], win_v[:, kc, c0:c0 + w], f"ld_x{xi}", [("xp", xi)])
            eng = "dve" if rr_alt.next() == 0 else "act"
            P.op(eng, fold_scale(eng, win[:, kc, c0:c0 + w], xp[xi][:, 0:w], gmix[:, kc:kc + 1]),
                 [("xp", xi), "gmix"], [("win", cb)])
    for kc in range(4):
        P.dma("pool", "w_a", lambda e, kc=kc: e.dma_start(out=wpa[:, kc, :], in_=wpa_d[kc * 128:(kc + 1) * 128, :]), reads=[("win", 4)], writes=["wpa"])
        P.dma("pool", "w_b", lambda e, kc=kc: e.dma_start(out=wpb[:, kc, :], in_=wpb_d[kc * 128:(kc + 1) * 128, :]), reads=[("win", 4)], writes=["wpb"])
    for kc in range(KC):
        P.dma("pool", "w_o", lambda e, kc=kc: e.dma_start(out=wout[:, kc, :], in_=wout_d[kc * 128:(kc + 1) * 128, :]), reads=[("win", 4)], writes=["wout"])

    def win_keys(c0, c1):
        return [("win", cb) for cb in range(5) if pieces[cb][0] < c1 and pieces[cb][0] + pieces[cb][1] > c0]

    prep_state = {}

    def norm_prep(x_ap, x_key, hbufs, hkey, tag):
        hi = rr_hb.next()
        si = rr_ss.next()
        P.op("act", lambda e, o=hbufs[hi], i=x_ap, a=ss[:, si:si + 1]: e.activation(o, i, AF.Square, accum_out=a),
             [x_key], [(hkey, hi), ("ss", si)])
        emit_rstd(si)
        P.op("act", lambda e, o=hbufs[hi], i=x_ap, m=rstd[:, si:si + 1]: e.mul(o, i, m),
             [x_key, ("rstd", si)], [(hkey, hi)])
        prep_state[tag] = hi

    def norm_trans(hbufs, hkey, tag, dst_hT, col0, hT_key):
        hi = prep_state.pop(tag)
        b = rr_bank.next()
        pst = bank_bf16(b)
        for kc in range(KC):
            P.op("pe", lambda e, o=pst[:, kc * 128:(kc + 1) * 128], i=hbufs[hi][:, kc * 128:(kc + 1) * 128]:
                 e.transpose(o, i, ident), [(hkey, hi), "ident"], [("bank", b)])
        eng = "dve" if rr_alt.next() == 0 else "act"
        evac_copy(eng, dst_hT[:, :, col0:col0 + 128], pst.rearrange("p (a b) -> p a b", a=KC),
                  [("bank", b)], [hT_key])

    def stage_prep1(s):
        for b in range(NB1):
            r0 = s * T1 + b * 128
            xi = rr_xp.next()
            dma_load(xp[xi], x_d[r0:r0 + 128, :], f"ld_x{xi}", [("xp", xi)])
            norm_prep(xp[xi], ("xp", xi), hb, "hb", ("p1", s, b))

    def stage_trans1(s):
        slot = s % 3
        for b in range(NB1):
            norm_trans(hb, "hb", ("p1", s, b), hT[slot], b * 128, ("hT", slot, b))

    def hT_keys(slot):
        return [("hT", slot, b) for b in range(NB1)]

    def stage_kv(s):
        slot = s % 3
        rs = s % 3
        kchunks = [KA // 128] + [KB // 128 + j for j in range(4)]
        for ki, f in enumerate(kchunks):
            b = rr_bank.next()
            ps = bank_f32(b)
            for kc in range(KC):
                P.op("pe", lambda e, o=ps[:, 0:T1], l=win[:, kc, f * 128:(f + 1) * 128], r=hT[slot][:, kc, :], kc=kc:
                     e.matmul(o, l, r, start=(kc == 0), stop=(kc == KC - 1)),
                     win_keys(f * 128, f * 128 + 128) + hT_keys(slot), [("bank", b)])
            eng = "act" if rr_alt.next() == 0 else "dve"
            evac_copy(eng, kT[rs][:, ki, :], ps[:, 0:T1], [("bank", b)], [("kT", rs, ki)], mul=0.125)
        for blk in range(NB1):
            b1 = rr_bank.next()
            b2 = rr_bank.next()
            psb = bank_f32(b1)
            psa = bank_f32(b2)
            for kc in range(KC):
                P.op("pe", lambda e, o=psb, l=hT[slot][:, kc, blk * 128:(blk + 1) * 128], r=win[:, kc, VB:VB + 512], kc=kc:
                     e.matmul(o, l, r, start=(kc == 0), stop=(kc == KC - 1)),
                     win_keys(VB, VB + 512) + hT_keys(slot), [("bank", b1)])
            for kc in range(KC):
                P.op("pe", lambda e, o=psa[:, 0:128], l=hT[slot][:, kc, blk * 128:(blk + 1) * 128], r=win[:, kc, VA:VA + 128], kc=kc:
                     e.matmul(o, l, r, start=(kc == 0), stop=(kc == KC - 1)),
                     win_keys(VA, VA + 128) + hT_keys(slot), [("bank", b2)])
            evac_copy("dve", vR[rs][:, blk, 2:10, 0:64], psb.rearrange("p (h d) -> p h d", h=8),
                      [("bank", b1)], [("v", rs, blk)])
            evac_copy("act", vR[rs][:, blk, 0:2, 0:64], psa[:, 0:128].rearrange("p (h d) -> p h d", h=2),
                      [("bank", b2)], [("v", rs, blk)])

    def stage_q(s):
        slot = s % 3
        qchunks = [QA // 128 + j for j in range(4)] + [QB // 128 + j for j in range(4)]
        for qi, f in enumerate(qchunks):
            b = rr_bank.next()
            ps = bank_f32(b)
            for kc in range(KC):
                P.op("pe", lambda e, o=ps[:, 0:T1], l=win[:, kc, f * 128:(f + 1) * 128], r=hT[slot][:, kc, :], kc=kc:
                     e.matmul(o, l, r, start=(kc == 0), stop=(kc == KC - 1)),
                     win_keys(f * 128, f * 128 + 128) + hT_keys(slot), [("bank", b)])
            eng = "act" if rr_alt.next() == 0 else "dve"
            evac_copy(eng, qT[:, qi, :], ps[:, 0:T1], [("bank", b)], [("qT", qi)])

    def b_tiles(i):
        if i in (0, 1):
            js = [0, 1, 2, 3]
            return [(j, ("s", m)) for m, j in enumerate(js)]
        if i in (NBLK - 2, NBLK - 1):
            js = [NBLK - 4, NBLK - 3, NBLK - 2, NBLK - 1]
            return [(j, ("s", m)) for m, j in enumerate(js)]
        return [(i + o, ("i", o + 2)) for o in range(-2, 3)]

    bmask_ctr = [0]
    special_ids = {0: 0, 1: 1, NBLK - 2: 2, NBLK - 1: 3}
    cur_special = [0]

    def ensure_special(i):
        sid = special_ids[i]
        if cur_special[0] != sid:
            mflat = maskBs.rearrange("p a b c -> p (a b c)")
            for c0 in range(0, 8 * 4 * 128, 1024):
                xi = rr_xp.next()
                dma_load(xp[xi], biasBs_d[sid][:, c0:c0 + 1024], f"ld_x{xi}", [("xp", xi)])
                P.op("act", lambda e, o=mflat[:, c0:c0 + 1024], i_=xp[xi]: e.activation(o, i_, AF.Exp),
                     [("xp", xi)], ["maskBs"])
            cur_special[0] = sid

    YBA = [0, 1]
    YBB = [2, 3]
    rr_sb = RR(2)
    rr_db = RR(2)
    SBK = [4, 5]
    DBK = [6, 7]

    def gen_attn(s):
        ytile = yT[s % 2]
        tasks = []
        for blk in range(NB1):
            i = s * NB1 + blk
            ja = [j for j in (i - 1, i, i + 1) if 0 <= j < NBLK]
            for g in range(2):
                for j in ja:
                    tasks.append(dict(kind="A", blk=blk, i=i, g=g, j=j, msel=j - i + 1,
                                      first=(j == ja[0]), last=(j == ja[-1]), grp_first=(g == 0 and j == ja[0]),
                                      grp_last=(g == 1 and j == ja[-1])))
            bt = b_tiles(i)
            for n, (j, msel) in enumerate(bt):
                for hbk in range(2):
                    tasks.append(dict(kind="B", blk=blk, i=i, g=hbk, j=j, msel=msel,
                                      first=(n == 0), last=(n == len(bt) - 1), grp_first=(n == 0 and hbk == 0),
                                      grp_last=(n == len(bt) - 1 and hbk == 1)))

        def emit_qk(t):
            g, j, blk = t["g"], t["j"], t["blk"]
            tq = slice(blk * 128, (blk + 1) * 128)
            rs = (j // NB1) % 3
            pos = j % NB1
            kcols = slice(pos * 128, (pos + 1) * 128)
            b = SBK[rr_sb.next()]
            ps = bank_f32(b)
            if t["kind"] == "A":
                pr = slice(64 * g, 64 * g + 64)
                P.op("pe", lambda e, o=ps.rearrange("p (a b) -> p a b", a=4), l=kT[rs][pr, 0, kcols], r=qT[pr, 0:4, tq]:
                     e.matmul(o, l, r, start=True, stop=True),
                     [("kT", rs, 0)] + [("qT", c) for c in range(4)], [("bank", b)])
            else:
                for hh in range(4):
                    h = 2 * hh + g
                    pr = slice(64 * (h % 2), 64 * (h % 2) + 64)
                    P.op("pe", lambda e, o=ps[:, hh * 128:(hh + 1) * 128], l=kT[rs][pr, 1 + h // 2, kcols], r=qT[pr, 4 + h // 2, tq]:
                         e.matmul(o, l, r, start=True, stop=True),
                         [("kT", rs, 1 + h // 2), ("qT", 4 + h // 2)], [("bank", b)])
            return b

        def emit_soft(t, b):
            g, msel = t["g"], t["msel"]
            ps = bank_f32(b)
            ui = rr_pu.next()
            ti = rr_pt.next()
            P.op("act", lambda e, o=pu[ui], i_=ps: e.activation(o, i_, AF.Exp), [("bank", b)], [("pu", ui)])
            if t["kind"] == "A":
                m_ap = maskA[:, 4 * g:4 * g + 4, msel, :]
                mkey = "maskA"
                eng = "dve"
            else:
                if msel[0] == "i":
                    m_ap = maskBi[:, g:8:2, msel[1], :]
                    mkey = "maskBi"
                else:
                    m_ap = maskBs[:, g:8:2, msel[1], :]
                    mkey = "maskBs"
                eng = "pool" if (bmask_ctr[0] % 5) != 4 else "dve"
                bmask_ctr[0] += 1
            P.op(eng, lambda e, o=pt[ti], a=pu[ui].rearrange("p (a b) -> p a b", a=4), m=m_ap:
                 e.tensor_tensor(o, a, m, ALU.mult), [("pu", ui), mkey], [("pt", ti)])
            return ti

        def ybanks(kind):
            return YBA if kind == "A" else YBB

        def emit_pv(t, ti):
            g, j = t["g"], t["j"]
            rs = (j // NB1) % 3
            pos = j % NB1
            yb = ybanks(t["kind"])[g]
            yps = bank_f32(yb)
            for c in range(4):
                vh = g if t["kind"] == "A" else 2 + 2 * c + g
                P.op("pe", lambda e, o=yps[:, c * 65:(c + 1) * 65], l=pt[ti][:, c, :], r=vR[rs][:, pos, vh, :]:
                     e.matmul(o, l, r, start=False, stop=t["last"], skip_group_check=True),
                     [("pt", ti), ("v", rs, pos)], [("bank", yb)])

        def emit_zero(kind):
            for yb_ in ybanks(kind):
                P.op("pe", lambda e, o=bank_f32(yb_)[:, 0:260]: e.matmul(o, zt[:, 0:128], zt[:, 0:260], start=True, stop=False,
                                                                          skip_group_check=True), ["zt"], [("bank", yb_)])

        def emit_norm(kind, ybuf):
            for g in range(2):
                yb = ybanks(kind)[g]
                yps = bank_f32(yb)[:, 0:260].rearrange("p (h d) -> p h d", h=4)
                di = rr_den.next()
                dn = den[:, di, :]
                if kind == "A":
                    P.op("dve", lambda e, o=dn, a=yps[:, :, 64], b_=es[:, 4 * g:4 * g + 4]:
                         e.tensor_tensor(o, a, b_, ALU.add), [("bank", yb), "es"], [("den", di)])
                    P.op("dve", lambda e, o=dn: e.reciprocal(o, o), [("den", di)], [("den", di)])
                    yo = ysb[ybuf][:, g * 256:(g + 1) * 256].rearrange("p (h d) -> p h d", h=4)
                else:
                    P.op("dve", lambda e, o=dn, a=yps[:, :, 64]: e.reciprocal(o, a), [("bank", yb)], [("den", di)])
                    yo = ysb[ybuf][:, 512:1024].rearrange("p (h d) -> p h d", h=8)[:, g:8:2, :]
                P.op("dve", lambda e, o=yo, a=yps[:, :, 0:64], r=dn:
                     e.tensor_tensor(o, a, r.rearrange("p (h o) -> p h o", o=1).broadcast_to([128, 4, 64]), ALU.mult),
                     [("bank", yb), ("den", di)], [("ysb", ybuf, kind)])

        def emit_ytrans(blk, i):
            ybuf = blk
            tq = slice(blk * 128, (blk + 1) * 128)
            if debug:
                P.dma("pool", f"dbg_y{ybuf}", lambda e, o=ydbg_d[i * 128:(i + 1) * 128, :], i_=ysb[ybuf]: e.dma_start(out=o, in_=i_),
                      reads=[("ysb", ybuf, "A"), ("ysb", ybuf, "B")], writes=[("ydbg", i)])
            b = SBK[rr_sb.next()]
            pst = bank_bf16(b)
            for fc in range(8):
                P.op("pe", lambda e, o=pst[:, fc * 128:(fc + 1) * 128], i_=ysb[ybuf][:, fc * 128:(fc + 1) * 128]:
                     e.transpose(o, i_, ident), [("ysb", ybuf, "A"), ("ysb", ybuf, "B"), "ident"], [("bank", b)])
            evac_copy("act", ytile[:, :, tq], pst.rearrange("p (a b) -> p a b", a=8), [("bank", b)], [("yT", s % 2, blk)])

        def retire(t, ti):
            if t["grp_first"]:
                emit_zero(t["kind"])
            emit_pv(t, ti)
            if t["grp_last"]:
                emit_norm(t["kind"], t["blk"])
                if t["kind"] == "B":
                    emit_ytrans(t["blk"], t["i"])

        LOOK = 4
        pend = []
        for t in tasks:
            if t["kind"] == "B" and t["grp_first"] and t["i"] in special_ids:
                ensure_special(t["i"])
            b = emit_qk(t)
            ti = emit_soft(t, b)
            pend.append((t, ti))
            yield
            if len(pend) > LOOK:
                retire(*pend.pop(0))
        while pend:
            retire(*pend.pop(0))
            yield

    def gen_proj(s):
        slot = s % 3
        ytile = yT[s % 2]
        yT_keys = [("yT", s % 2, b) for b in range(NB1)]
        for c in range(8):
            bg = DBK[rr_db.next()]
            psg = bank_f32(bg)
            for half, goff in enumerate((GA, GB)):
                for kc in range(KC):
                    P.op("pe", lambda e, o=psg[:, half * T1:(half + 1) * T1], l=win[:, kc, goff + c * 128:goff + (c + 1) * 128],
                         r=hT[slot][:, kc, :], kc=kc: e.matmul(o, l, r, start=(kc == 0), stop=(kc == KC - 1)),
                         win_keys(goff + c * 128, goff + (c + 1) * 128) + hT_keys(slot), [("bank", bg)])
                yield
            gi = rr_alt.next()
            for half in range(2):
                P.op("act", lambda e, o=sg[gi][:, half * T1:(half + 1) * T1], i_=psg[:, half * T1:(half + 1) * T1],
                     bb=nbgate[:, half * 8 + c:half * 8 + c + 1]: e.activation(o, i_, AF.Exp, bias=bb, scale=-1.0),
                     [("bank", bg), "nbgate"], [("sg", gi)])
            bu = DBK[rr_db.next()]
            psu = bank_f32(bu)
            for half, wp in enumerate((wpa, wpb)):
                for fc in range(4):
                    P.op("pe", lambda e, o=psu[:, half * T1:(half + 1) * T1], l=wp[:, fc, c * 128:(c + 1) * 128],
                         r=ytile[:, half * 4 + fc, :], fc=fc: e.matmul(o, l, r, start=(fc == 0), stop=(fc == 3)),
                         ["wpa", "wpb"] + yT_keys, [("bank", bu)])
                if half == 0:
                    yield
            P.op("act", lambda e, o=sg[gi]: e.activation(o, o, AF.Ln, bias=ones1[:, 0:1]), [("sg", gi), "ones1"], [("sg", gi)])
            P.op("act", lambda e, o=sg[gi]: e.activation(o, o, AF.Exp, scale=-1.0), [("sg", gi)], [("sg", gi)])
            P.op("dve", lambda e, o=sg[gi], a=sg[gi], u=psu: e.tensor_tensor(o, a, u, ALU.mult),
                 [("sg", gi), ("bank", bu)], [("sg", gi)])
            P.op("pool", lambda e, o=mT[:, c, :], a=sg[gi][:, 0:T1], b_=sg[gi][:, T1:2 * T1]:
                 e.tensor_tensor(o, a, b_, ALU.add), [("sg", gi)], [("mT", c)])
            yield

    def gen_out(s):
        mT_keys = [("mT", c) for c in range(8)]
        if debug:
            P.dma("pool", "dbg_m", lambda e, o=mdbg_d.rearrange("(c p) t -> p c t", p=128)[:, :, s * T1:(s + 1) * T1], i_=mT:
                  e.dma_start(out=o, in_=i_), reads=mT_keys, writes=[("mdbg", s)])
        for blk in range(NB1):
            r0 = s * T1 + blk * 128
            xi = rr_xp.next()
            dma_load(xp[xi], x_d[r0:r0 + 128, :], f"ld_x{xi}", [("xp", xi)])
            for half in range(2):
                b = DBK[rr_db.next()]
                ps = bank_f32(b)
                for c in range(8):
                    P.op("pe", lambda e, o=ps, l=mT[:, c, blk * 128:(blk + 1) * 128], r=wout[:, c, half * 512:(half + 1) * 512], c=c:
                         e.matmul(o, l, r, start=(c == 0), stop=(c == 7)), mT_keys + ["wout"], [("bank", b)])
                    if c == 3:
                        yield
                P.op("dve", lambda e, o=xp[xi][:, half * 512:(half + 1) * 512], a=ps:
                     e.tensor_tensor(o, a, o, ALU.add), [("bank", b), ("xp", xi)], [("xp", xi)])
                yield
            P.dma("sp", f"st_x{xi}", lambda e, o=x1_d[r0:r0 + 128, :], i_=xp[xi]: e.dma_start(out=o, in_=i_),
                  reads=[("xp", xi)], writes=[("x1", r0 // 128)])

    def gen_chain(*gens):
        for g_ in gens:
            yield from g_

    def interleave(ga, gd):
        a_done = d_done = False
        while not (a_done and d_done):
            if not a_done:
                try:
                    next(ga)
                except StopIteration:
                    a_done = True
            if not d_done:
                try:
                    next(gd)
                except StopIteration:
                    d_done = True

    stage_prep1(0)
    stage_trans1(0)
    stage_prep1(1)
    stage_kv(0)
    for s in range(NS1):
        if s + 1 < NS1:
            stage_trans1(s + 1)
            if s + 2 < NS1:
                stage_prep1(s + 2)
            stage_kv(s + 1)
        stage_q(s)
        dense = gen_chain(gen_proj(s - 1), gen_out(s - 1)) if s >= 1 else iter(())
        interleave(gen_attn(s), dense)
    interleave(iter(()), gen_chain(gen_proj(NS1 - 1), gen_out(NS1 - 1)))
    if stop_after is not None:
        P.wait_all("sp")
        P.replay(nc)
        return nc

    P.barrier()
    AR.reset(base_mark)
    wup = AR.alloc(BF16, [KC, DFF])
    wdn = AR.alloc(BF16, [32, D])
    NXB = 5
    xb = [AR.alloc(F32, [D]) for _ in range(NXB)]
    NHB2 = 4
    hb2 = [AR.alloc(BF16, [D]) for _ in range(NHB2)]
    junk2 = AR.alloc(BF16, [D])
    h2T = AR.alloc(BF16, [KC, T2])
    aT = AR.alloc(BF16, [32, T2])
    NRT = 3
    rt = [AR.alloc(BF16, [T2]) for _ in range(NRT)]
    gfin = AR.alloc(F32, [D])
    rr_xb = RR(NXB)
    rr_bank = RR(8)
    rr_rt = RR(NRT)
    rr_hb2 = RR(NHB2)

    dma_load(gfin, gfin_d.partition_broadcast(128), "c_gfin", ["gfin"])
    wup_v = wup_d.rearrange("(kc p) n -> p kc n", p=128)
    for cb in range(4):
        c0 = cb * 1024
        for kc in range(KC):
            xi = rr_xb.next()
            dma_load(xb[xi], wup_v[:, kc, c0:c0 + 1024], f"ld_b{xi}", [("xb", xi)])
            eng = "dve" if rr_alt.next() == 0 else "act"
            P.op(eng, fold_scale(eng, wup[:, kc, c0:c0 + 1024], xb[xi], gmlp[:, kc:kc + 1]),
                 [("xb", xi), "gmlp"], [("wup", cb)])

    for fc in range(32):
        P.dma("pool", f"w_d{fc // 4}", lambda e, fc=fc: e.dma_start(out=wdn[:, fc, :], in_=wdn_d[fc * 128:(fc + 1) * 128, :]), reads=[("wup", 2)], writes=[("wdn", fc // 4)])
    hb2keys = "hb2"

    def prep2(u, blk):
        r0 = u * T2 + blk * 128
        xi = rr_xb.next()
        dma_load(xb[xi], x1_d[r0:r0 + 128, :], f"ld_b{xi}", [("xb", xi)], rkeys=[("x1", r0 // 128)])
        norm_prep2(xb[xi], ("xb", xi), ("p2", u, blk))
        return xi

    def norm_prep2(x_ap, x_key, tag):
        hi = rr_hb2.next()
        si = rr_ss.next()
        P.op("act", lambda e, o=hb2[hi], i_=x_ap, a=ss[:, si:si + 1]: e.activation(o, i_, AF.Square, accum_out=a),
             [x_key], [("hb2", hi), ("ss", si)])
        emit_rstd(si)
        P.op("act", lambda e, o=hb2[hi], i_=x_ap, m=rstd[:, si:si + 1]: e.mul(o, i_, m),
             [x_key, ("rstd", si)], [("hb2", hi)])
        prep_state[tag] = hi

    def trans2(u, blk):
        hi = prep_state.pop(("p2", u, blk))
        b = rr_bank.next()
        pst = bank_bf16(b)
        for kc in range(KC):
            P.op("pe", lambda e, o=pst[:, kc * 128:(kc + 1) * 128], i_=hb2[hi][:, kc * 128:(kc + 1) * 128]:
                 e.transpose(o, i_, ident), [("hb2", hi), "ident"], [("bank", b)])
        eng = "dve" if rr_alt.next() == 0 else "act"
        evac_copy(eng, h2T[:, :, blk * 128:(blk + 1) * 128], pst.rearrange("p (a b) -> p a b", a=KC),
                  [("bank", b)], [("h2T", blk)])

    h2T_keys = [("h2T", b) for b in range(NB2)]
    xis = {}
    for blk in range(NB2):
        xis[(0, blk)] = prep2(0, blk)
    for blk in range(NB2):
        trans2(0, blk)
    for u in range(NS2):
        for fc in range(32):
            b = rr_bank.next()
            ps = bank_f32(b)
            for kc in range(KC):
                P.op("pe", lambda e, o=ps, l=wup[:, kc, fc * 128:(fc + 1) * 128], r=h2T[:, kc, :], kc=kc:
                     e.matmul(o, l, r, start=(kc == 0), stop=(kc == KC - 1)), [("wup", fc // 8)] + h2T_keys, [("bank", b)])
            ri = rr_rt.next()
            P.op("act", lambda e, o=rt[ri], i_=ps: e.activation(o, i_, AF.Relu), [("bank", b)], [("rt", ri)])
            P.op("dve", lambda e, o=aT[:, fc, :], a=rt[ri]: e.tensor_tensor(o, a, a, ALU.mult), [("rt", ri)], [("aT", fc)])
        if u + 1 < NS2:
            xis[(u + 1, 0)] = prep2(u + 1, 0)
        for blk in range(NB2):
            xi = xis[(u, blk)]
            r0 = u * T2 + blk * 128
            for half in range(2):
                b = rr_bank.next()
                ps = bank_f32(b)
                for fc in range(32):
                    P.op("pe", lambda e, o=ps, l=aT[:, fc, blk * 128:(blk + 1) * 128], r=wdn[:, fc, half * 512:(half + 1) * 512], fc=fc:
                         e.matmul(o, l, r, start=(fc == 0), stop=(fc == 31)), [("aT", fc), ("wdn", fc // 4)], [("bank", b)])
                P.op("dve", lambda e, o=xb[xi][:, half * 512:(half + 1) * 512], a=ps:
                     e.tensor_tensor(o, a, o, ALU.add), [("bank", b), ("xb", xi)], [("xb", xi)])
            si = rr_ss.next()
            P.op("act", lambda e, o=junk2, i_=xb[xi], a=ss[:, si:si + 1]: e.activation(o, i_, AF.Square, accum_out=a),
                 [("xb", xi)], ["junk2", ("ss", si)])
            emit_rstd(si)
            P.op("dve", lambda e, o=xb[xi], m=rstd[:, si:si + 1]:
                 e.scalar_tensor_tensor(o, o, m, gfin, ALU.mult, ALU.mult), [("xb", xi), ("rstd", si), "gfin"], [("xb", xi)])
            P.dma("sp", f"st_b{xi}", lambda e, o=out_d[r0:r0 + 128, :], i_=xb[xi]: e.dma_start(out=o, in_=i_),
                  reads=[("xb", xi)], writes=[("out", r0 // 128)])
            if u + 1 < NS2 and blk + 1 < NB2:
                xis[(u + 1, blk + 1)] = prep2(u + 1, blk + 1)
        if u + 1 < NS2:
            for blk in range(NB2):
                trans2(u + 1, blk)
    P.wait_all("sp")
    P.replay(nc)
    return nc


def _perm_cols():
    cols = []
    for c in range(4):
        cols += list(range(c * 64, (c + 1) * 64))
        cols += list(range((4 + c) * 64, (5 + c) * 64))
    cols += list(range(512, 640))
    cols += list(range(768, 1280))
    cols += list(range(1280, 1792))
    cols += list(range(640, 768))
    cols += list(range(1792, 2304))
    cols += list(range(2304, 4352))
    return np.array(cols)


def _bias_a():
    k = np.arange(128)[:, None]
    q = np.arange(128)[None, :]
    out = np.full((128, 8, 3, 128), NEG, np.float32)
    for h in range(8):
        slope = 2.0 ** (-(h + 1))
        for sl in range(3):
            dist = np.abs((sl - 1) * 128 + k - q)
            out[:, h, sl, :] = np.where(dist <= 128, -slope * dist, NEG)
    return out.reshape(128, -1)


def _bias_b_gather(rpb, rows, i, js):
    out = np.full((128, 8, len(js), 128), NEG, np.float32)
    kk = np.arange(128)
    a, ck = kk // 64, kk % 64
    b, cq = kk // 64, kk % 64
    cs = np.clip(cq - 8, 0, 48)
    r = 2 * i + b
    r0 = np.clip(r - 4, 0, rows - 8)
    for m, j in enumerate(js):
        R = 2 * j + a
        vr = (R[:, None] >= r0[None, :]) & (R[:, None] <= r0[None, :] + 7)
        vc = (ck[:, None] >= cs[None, :]) & (ck[:, None] <= cs[None, :] + 15)
        valid = vr & vc
        dr = np.clip(R[:, None] - r[None, :] + 7, 0, 14)
        dc = np.clip(ck[:, None] - cq[None, :] + 15, 0, 30)
        g = rpb[:, dr, dc]
        g = np.where(valid[None], g, np.float32(NEG))
        out[:, :, m, :] = np.transpose(g, (1, 0, 2))
    return out


_NC_CACHE = {}


def _get_nc(S, debug=False):
    key = (S, debug)
    if key not in _NC_CACHE:
        _NC_CACHE[key] = build_program(S, debug)
    return _NC_CACHE[key]


def make_in_maps(x, norm_mix, w_in, b_gate, sink, rpb, w_proj_a, w_proj_b, w_out,
                 norm_mlp, w_up, w_down, norm_final):
    f = lambda a: np.ascontiguousarray(np.asarray(a, dtype=np.float32))
    x = f(x)
    B, S, _ = x.shape
    rows = S // 64
    NBLK = S // 128
    rpb0 = f(rpb)[0]
    common = {
        "w_in": f(f(w_in)[0][:, _perm_cols()]),
        "w_proj_a": f(w_proj_a)[0], "w_proj_b": f(w_proj_b)[0], "w_out": f(w_out)[0],
        "w_up": f(w_up)[0], "w_down": f(w_down)[0],
        "gmix": f(f(norm_mix)[0].reshape(KC, 128).T),
        "gmlp": f(f(norm_mlp)[0].reshape(KC, 128).T),
        "bgate": f(f(b_gate)[0].reshape(16, 128).T),
        "sink": f(sink)[0], "gfin": f(norm_final),
        "ident": np.eye(128, dtype=np.float32),
        "biasA": _bias_a(),
        "biasBi": f(_bias_b_gather(rpb0, rows, 2, [0, 1, 2, 3, 4]).reshape(128, -1)),
        "biasBs": f(np.stack([
            _bias_b_gather(rpb0, rows, 0, [0, 1, 2, 3]).reshape(128, -1),
            _bias_b_gather(rpb0, rows, 1, [0, 1, 2, 3]).reshape(128, -1),
            _bias_b_gather(rpb0, rows, NBLK - 2, [NBLK - 4, NBLK - 3, NBLK - 2, NBLK - 1]).reshape(128, -1),
            _bias_b_gather(rpb0, rows, NBLK - 1, [NBLK - 4, NBLK - 3, NBLK - 2, NBLK - 1]).reshape(128, -1),
        ])),
    }
    return [dict(common, x=f(x[b])) for b in range(B)]


def kernel(x, norm_mix, w_in, b_gate, sink, rpb, w_proj_a, w_proj_b, w_out,
           norm_mlp, w_up, w_down, norm_final):
    in_maps = make_in_maps(x, norm_mix, w_in, b_gate, sink, rpb, w_proj_a, w_proj_b, w_out,
                           norm_mlp, w_up, w_down, norm_final)
    B = len(in_maps)
    S = in_maps[0]["x"].shape[0]
    nc = _get_nc(S)
    res = run_bass_kernel_spmd(nc, in_maps, core_ids=list(range(B)))
    return np.stack([np.asarray(r["out"], dtype=np.float32) for r in res.results], axis=0)
```

```python
import os
import numpy as np
import concourse.bass as bass
import concourse.mybir as mybir
from concourse.bass_utils import run_bass_kernel_spmd

F32 = mybir.dt.float32
BF16 = mybir.dt.bfloat16
AF = mybir.ActivationFunctionType
ALU = mybir.AluOpType

D = 1024
KC = 8
NIN = 4352
DFF = 4096
QA, KA, QB, KB, VA, VB, GA, GB = 0, 512, 640, 1152, 1664, 1792, 2304, 3328
NEG = -30000.0
EPS = 1e-6
T1 = 256
T2 = 512
ENGS = ("pe", "act", "dve", "pool", "sp")


class Prog:
    def __init__(self, same_engine_sync=True):
        self.ops = {e: [] for e in ENGS}
        self.ncomp = {e: 0 for e in ENGS}
        self.chan_cnt = {}
        self.lastw = {}
        self.readers = {}
        self.waited = {e: {} for e in ENGS}
        self.signals = set()
        self.same_engine_sync = same_engine_sync

    def _deps(self, reads, writes):
        deps = {}
        for r in reads:
            w = self.lastw.get(r)
            if w is not None:
                deps[w] = True
        for w_ in writes:
            w = self.lastw.get(w_)
            if w is not None:
                deps.setdefault(w, False)
            for c, i in self.readers.get(w_, {}).items():
                deps.setdefault((c, i), False)
        return deps

    def _waits(self, eng, deps):
        if not isinstance(deps, dict):
            deps = {d_: True for d_ in deps}
        best = {}
        for (c, i), raw in deps.items():
            if c == eng and (eng == "pe" or not self.same_engine_sync or not raw):
                continue
            if best.get(c, 0) < i:
                best[c] = i
        waits = []
        for c, i in best.items():
            if self.waited[eng].get(c, 0) >= i:
                continue
            self.waited[eng][c] = i
            waits.append((c, i))
            if not c.startswith("dma:"):
                self.signals.add((c, i))
        return waits

    def _commit(self, counter, idx, reads, writes):
        for r in reads:
            self.readers.setdefault(r, {})[counter] = idx
        for w_ in writes:
            self.lastw[w_] = (counter, idx)
            self.readers[w_] = {}

    def op(self, eng, fn, reads=(), writes=()):
        waits = self._waits(eng, self._deps(reads, writes))
        self.ncomp[eng] += 1
        idx = self.ncomp[eng]
        self.ops[eng].append(dict(fn=fn, waits=waits, kind="c", idx=idx))
        self._commit(eng, idx, reads, writes)

    def dma(self, queue, chan, fn, reads=(), writes=()):
        waits = self._waits(queue, self._deps(reads, writes))
        c = "dma:" + chan
        self.chan_cnt[c] = self.chan_cnt.get(c, 0) + 1
        idx = self.chan_cnt[c]
        self.ops[queue].append(dict(fn=fn, waits=waits, kind="d", chan=c))
        self._commit(c, idx, reads, writes)

    def wait_all(self, eng, include_dma=True):
        deps = set()
        for e in ENGS:
            if e != eng and self.ncomp[e] > 0:
                deps.add((e, self.ncomp[e]))
        if include_dma:
            for c, n in self.chan_cnt.items():
                deps.add((c, n))
        waits = self._waits(eng, deps)
        self.ops[eng].append(dict(fn=None, waits=waits, kind="w"))

    def barrier(self):
        snap = {e: self.ncomp[e] for e in ENGS}
        chans = dict(self.chan_cnt)
        for eng in ENGS:
            deps = set((e, n) for e, n in snap.items() if e != eng and n > 0)
            deps |= set(chans.items())
            waits = self._waits(eng, deps)
            self.ops[eng].append(dict(fn=None, waits=waits, kind="w"))

    def replay(self, nc):
        chan_names = sorted(self.chan_cnt.keys())
        semval = {}
        for e in ENGS:
            c = 0
            for i in range(1, self.ncomp[e] + 1):
                if (e, i) in self.signals:
                    c += 1
                semval[(e, i)] = c
        import contextlib
        with contextlib.ExitStack() as st:
            sems = {}
            for e in ENGS:
                sems[e] = st.enter_context(nc.semaphore("s_" + e))
            for c in chan_names:
                sems[c] = st.enter_context(nc.semaphore("s_" + c.replace(":", "_")))
            block = st.enter_context(nc.Block())

            def run(engname, engobj):
                for o in self.ops[engname]:
                    for (c, i) in o["waits"]:
                        if c.startswith("dma:"):
                            engobj.wait_ge(sems[c], 16 * i)
                        else:
                            engobj.wait_ge(sems[c], semval[(c, i)])
                    if o["fn"] is None:
                        continue
                    ins = o["fn"](engobj)
                    if o["kind"] == "d":
                        ins.then_inc(sems[o["chan"]], 16)
                    elif (engname, o["idx"]) in self.signals:
                        ins.then_inc(sems[engname], 1)

            @block.tensor
            def _(e):
                run("pe", e)

            @block.scalar
            def _(e):
                run("act", e)

            @block.vector
            def _(e):
                run("dve", e)

            @block.gpsimd
            def _(e):
                run("pool", e)

            @block.sync
            def _(e):
                run("sp", e)


class Arena:
    def __init__(self, handle, nfloats):
        self.h = handle
        self.n = nfloats
        self.off = 0

    def mark(self):
        return self.off

    def reset(self, m):
        self.off = m

    def alloc(self, dtype, shape):
        nel = int(np.prod(shape))
        nbytes = nel * (2 if dtype == BF16 else 4)
        nfl = (nbytes + 3) // 4
        nfl = (nfl + 7) // 8 * 8
        assert self.off + nfl <= self.n, f"SBUF arena overflow: need {self.off + nfl} > {self.n}"
        ap = self.h[:, self.off:self.off + nfl]
        self.off += nfl
        if dtype == BF16:
            ap = ap.bitcast(BF16)
        ap = ap[:, 0:nel]
        if len(shape) == 2:
            ap = ap.rearrange("p (a b) -> p a b", a=shape[0])
        elif len(shape) == 3:
            ap = ap.rearrange("p (a b c) -> p a b c", a=shape[0], b=shape[1])
        elif len(shape) == 4:
            ap = ap.rearrange("p (a b c d) -> p a b c d", a=shape[0], b=shape[1], c=shape[2])
        return ap


class _Stop(Exception):
    pass


def build_program(S, debug=False, stop_after=None):
    assert S % T2 == 0 and S >= 1024
    NBLK = S // 128
    NS1 = S // T1
    NS2 = S // T2
    NB1 = T1 // 128
    NB2 = T2 // 128

    nc = bass.Bass("TRN2", target_bir_lowering=False)

    def din(name, shape):
        return nc.dram_tensor(name, list(shape), F32, kind="ExternalInput").ap()

    x_d = din("x", [S, D])
    win_d = din("w_in", [D, NIN])
    wpa_d = din("w_proj_a", [512, D])
    wpb_d = din("w_proj_b", [512, D])
    wout_d = din("w_out", [D, D])
    wup_d = din("w_up", [D, DFF])
    wdn_d = din("w_down", [DFF, D])
    gmix_d = din("gmix", [128, KC])
    gmlp_d = din("gmlp", [128, KC])
    bgate_d = din("bgate", [128, 16])
    sink_d = din("sink", [8])
    gfin_d = din("gfin", [D])
    ident_d = din("ident", [128, 128])
    biasA_d = din("biasA", [128, 8 * 3 * 128])
    biasBi_d = din("biasBi", [128, 8 * 5 * 128])
    biasBs_d = din("biasBs", [4, 128, 8 * 4 * 128])
    out_d = nc.dram_tensor("out", [S, D], F32, kind="ExternalOutput").ap()
    x1_d = nc.dram_tensor("x1s", [S, D], F32, kind="ExternalOutput" if debug else "Internal").ap()

    if debug:
        ydbg_d = nc.dram_tensor("ydbg", [S, D], F32, kind="ExternalOutput").ap()
        mdbg_d = nc.dram_tensor("mdbg", [D, S], F32, kind="ExternalOutput").ap()
    P = Prog()

    NFL = 212800 // 4
    arena_h = nc.alloc_sbuf_tensor("arena", [128, NFL], F32)
    AR = Arena(arena_h, NFL)
    banks = [nc.alloc_psum_tensor(f"ps{i}", [128, 512], F32) for i in range(8)]

    def bank_f32(b):
        return banks[b][:, :]

    def bank_bf16(b):
        return banks[b][:, :].bitcast(BF16)

    ident = AR.alloc(BF16, [128])
    gmix = AR.alloc(F32, [KC])
    gmlp = AR.alloc(F32, [KC])
    bgate = AR.alloc(F32, [16])
    nbgate = AR.alloc(F32, [16])
    ones1 = AR.alloc(F32, [8])
    es = AR.alloc(F32, [8])
    sinkb = AR.alloc(F32, [8])
    ss = AR.alloc(F32, [16])
    rstd = AR.alloc(F32, [16])
    den = AR.alloc(F32, [8, 4])
    base_mark = AR.mark()

    win = AR.alloc(BF16, [KC, NIN])
    wpa = AR.alloc(BF16, [4, D])
    wpb = AR.alloc(BF16, [4, D])
    wout = AR.alloc(BF16, [KC, D])
    hT = [AR.alloc(BF16, [KC, T1]) for _ in range(3)]
    kT = [AR.alloc(BF16, [5, T1]) for _ in range(3)]
    vR = [AR.alloc(BF16, [NB1, 10, 65]) for _ in range(3)]
    qT = AR.alloc(BF16, [8, T1])
    maskA = AR.alloc(BF16, [8, 3, 128])
    maskBi = AR.alloc(BF16, [8, 5, 128])
    maskBs = AR.alloc(BF16, [8, 4, 128])
    NPU, NPT = 4, 6
    pu = [AR.alloc(BF16, [512]) for _ in range(NPU)]
    pt = [AR.alloc(BF16, [4, 128]) for _ in range(NPT)]
    ysb = [AR.alloc(BF16, [D]) for _ in range(2)]
    yT = [AR.alloc(BF16, [8, T1]) for _ in range(2)]
    sg = [AR.alloc(F32, [512]) for _ in range(2)]
    mT = AR.alloc(BF16, [KC, T1])
    NXP = 4
    xp = [AR.alloc(F32, [D]) for _ in range(NXP)]
    hb = [AR.alloc(BF16, [D]) for _ in range(2)]
    zt = AR.alloc(BF16, [512])

    class RR:
        def __init__(self, n):
            self.n, self.i = n, 0

        def next(self):
            v = self.i % self.n
            self.i += 1
            return v

    rr_xp = RR(NXP)
    rr_hb = RR(2)
    rr_ss = RR(16)
    class RRB:
        def __init__(self):
            self.i = 0

        def next(self):
            v = self.i % 8
            self.i += 1
            return v

    rr_bank = RRB()
    rr_pu = RR(NPU)
    rr_pt = RR(NPT)
    rr_den = RR(8)
    rr_alt = RR(2)

    def emit_rstd(si):
        r = rstd[:, si:si + 1]
        P.op("dve", lambda e, o=r, i=ss[:, si:si + 1]: e.tensor_scalar(o, i, 1.0 / D, EPS, ALU.mult, ALU.add),
             [("ss", si)], [("rstd", si)])
        P.op("act", lambda e, o=r: e.activation(o, o, AF.Ln), [("rstd", si)], [("rstd", si)])
        P.op("act", lambda e, o=r: e.activation(o, o, AF.Exp, scale=-0.5), [("rstd", si)], [("rstd", si)])

    def dma_load(buf_ap, src_ap, chan, wkeys, rkeys=(), queue="sp"):
        P.dma(queue, chan, lambda e, o=buf_ap, i=src_ap: e.dma_start(out=o, in_=i), reads=rkeys, writes=wkeys)

    def evac_copy(eng, out_ap, in_ap, rkeys, wkeys, mul=None):
        if eng == "act":
            if mul is None:
                P.op("act", lambda e, o=out_ap, i=in_ap: e.copy(o, i), rkeys, wkeys)
            else:
                P.op("act", lambda e, o=out_ap, i=in_ap, m=mul: e.mul(o, i, m), rkeys, wkeys)
        else:
            if mul is None:
                P.op(eng, lambda e, o=out_ap, i=in_ap: e.tensor_copy(o, i), rkeys, wkeys)
            else:
                P.op(eng, lambda e, o=out_ap, i=in_ap, m=mul: e.tensor_scalar(o, i, m, None, ALU.mult), rkeys, wkeys)

    dma_load(gmix, gmix_d, "c_gmix", ["gmix"])
    dma_load(gmlp, gmlp_d, "c_gmlp", ["gmlp"])
    dma_load(bgate, bgate_d, "c_bgate", ["bgate"])
    dma_load(sinkb, sink_d.partition_broadcast(128), "c_sink", ["sinkb"])
    P.dma("pool", "c_ident", lambda e: e.dma_start(out=ident, in_=ident_d), writes=["ident"])
    P.op("act", lambda e: e.activation(es, sinkb, AF.Exp), ["sinkb"], ["es"])
    P.op("dve", lambda e: e.tensor_scalar(nbgate, bgate, -1.0, None, ALU.mult), ["bgate"], ["nbgate"])
    P.op("dve", lambda e: e.memset(ones1, 1.0), [], ["ones1"])
    P.op("pool", lambda e: e.memset(zt, 0.0), [], ["zt"])
    for r in range(3):
        P.op("pool", lambda e, r=r: e.memset(vR[r].rearrange("p a b c -> p (a b c)"), 1.0), [], [("v", r, b) for b in range(NB1)])

    def load_mask(mask_ap, src_ap, ncols, key):
        mflat = mask_ap.rearrange("p a b c -> p (a b c)")
        for c0 in range(0, ncols, 1024):
            xi = rr_xp.next()
            dma_load(xp[xi], src_ap[:, c0:c0 + 1024], f"ld_x{xi}", [("xp", xi)])
            P.op("act", lambda e, o=mflat[:, c0:c0 + 1024], i=xp[xi]: e.activation(o, i, AF.Exp),
                 [("xp", xi)], [key])

    load_mask(maskA, biasA_d, 8 * 3 * 128, "maskA")
    load_mask(maskBi, biasBi_d, 8 * 5 * 128, "maskBi")
    load_mask(maskBs, biasBs_d[0], 8 * 4 * 128, "maskBs")

    win_v = win_d.rearrange("(kc p) n -> p kc n", p=128)
    pieces = [(0, 1024), (1024, 1024), (2048, 1024), (3072, 1024), (4096, 256)]

    def fold_scale(eng, o, i_, g):
        if eng == "act":
            P_fn = lambda e, o=o, i_=i_, g=g: e.mul(o, i_, g)
        else:
            P_fn = lambda e, o=o, i_=i_, g=g: e.tensor_scalar(o, i_, g, None, ALU.mult)
        return P_fn

    for cb in (1, 0, 2, 3, 4):
        c0, w = pieces[cb]
        for kc in range(KC):
            xi = rr_xp.next()
            dma_load(xp[xi][:, 0:w], win_v[:, kc, c0:c0 + w], f"ld_x{xi}", [("xp", xi)])
            eng = "dve" if rr_alt.next() == 0 else "act"
            P.op(eng, fold_scale(eng, win[:, kc, c0:c0 + w], xp[xi][:, 0:w], gmix[:, kc:kc + 1]),
                 [("xp", xi), "gmix"], [("win", cb)])
    for kc in range(4):
        P.dma("pool", "w_a", lambda e, kc=kc: e.dma_start(out=wpa[:, kc, :], in_=wpa_d[kc * 128:(kc + 1) * 128, :]), reads=[("win", 4)], writes=["wpa"])
        P.dma("pool", "w_b", lambda e, kc=kc: e.dma_start(out=wpb[:, kc, :], in_=wpb_d[kc * 128:(kc + 1) * 128, :]), reads=[("win", 4)], writes=["wpb"])
    for kc in range(KC):
        P.dma("pool", "w_o", lambda e, kc=kc: e.dma_start(out=wout[:, kc, :], in_=wout_d[kc * 128:(kc + 1) * 128, :]), reads=[("win", 4)], writes=["wout"])

    def win_keys(c0, c1):
        return [("win", cb) for cb in range(5) if pieces[cb][0] < c1 and pieces[cb][0] + pieces[cb][1] > c0]

    prep_state = {}

    def norm_prep(x_ap, x_key, hbufs, hkey, tag):
        hi = rr_hb.next()
        si = rr_ss.next()
        P.op("act", lambda e, o=hbufs[hi], i=x_ap, a=ss[:, si:si + 1]: e.activation(o, i, AF.Square, accum_out=a),
             [x_key], [(hkey, hi), ("ss", si)])
        emit_rstd(si)
        P.op("act", lambda e, o=hbufs[hi], i=x_ap, m=rstd[:, si:si + 1]: e.mul(o, i, m),
             [x_key, ("rstd", si)], [(hkey, hi)])
        prep_state[tag] = hi

    def norm_trans(hbufs, hkey, tag, dst_hT, col0, hT_key):
        hi = prep_state.pop(tag)
        b = rr_bank.next()
        pst = bank_bf16(b)
        for kc in range(KC):
            P.op("pe", lambda e, o=pst[:, kc * 128:(kc + 1) * 128], i=hbufs[hi][:, kc * 128:(kc + 1) * 128]:
                 e.transpose(o, i, ident), [(hkey, hi), "ident"], [("bank", b)])
        eng = "dve" if rr_alt.next() == 0 else "act"
        evac_copy(eng, dst_hT[:, :, col0:col0 + 128], pst.rearrange("p (a b) -> p a b", a=KC),
                  [("bank", b)], [hT_key])

    def stage_prep1(s):
        for b in range(NB1):
            r0 = s * T1 + b * 128
            xi = rr_xp.next()
            dma_load(xp[xi], x_d[r0:r0 + 128, :], f"ld_x{xi}", [("xp", xi)])
            norm_prep(xp[xi], ("xp", xi), hb, "hb", ("p1", s, b))

    def stage_trans1(s):
        slot = s % 3
        for b in range(NB1):
            norm_trans(hb, "hb", ("p1", s, b), hT[slot], b * 128, ("hT", slot, b))

    def hT_keys(slot):
        return [("hT", slot, b) for b in range(NB1)]

    def stage_kv(s):
        slot = s % 3
        rs = s % 3
        for blk in range(NB1):
            b1 = rr_bank.next()
            b2 = rr_bank.next()
            psb = bank_f32(b1)
            psa = bank_f32(b2)
            for kc in range(KC):
                P.op("pe", lambda e, o=psb, l=hT[slot][:, kc, blk * 128:(blk + 1) * 128], r=win[:, kc, VB:VB + 512], kc=kc:
                     e.matmul(o, l, r, start=(kc == 0), stop=(kc == KC - 1)),
                     win_keys(VB, VB + 512) + [("hT", slot, blk)], [("bank", b1)])
            for kc in range(KC):
                P.op("pe", lambda e, o=psa[:, 0:128], l=hT[slot][:, kc, blk * 128:(blk + 1) * 128], r=win[:, kc, VA:VA + 128], kc=kc:
                     e.matmul(o, l, r, start=(kc == 0), stop=(kc == KC - 1)),
                     win_keys(VA, VA + 128) + [("hT", slot, blk)], [("bank", b2)])
            evac_copy("dve", vR[rs][:, blk, 2:10, 0:64], psb.rearrange("p (h d) -> p h d", h=8),
                      [("bank", b1)], [("v", rs, blk)])
            evac_copy("act", vR[rs][:, blk, 0:2, 0:64], psa[:, 0:128].rearrange("p (h d) -> p h d", h=2),
                      [("bank", b2)], [("v", rs, blk)])
        kchunks = [KA // 128] + [KB // 128 + j for j in range(4)]
        for ki, f in enumerate(kchunks):
            b = rr_bank.next()
            ps = bank_f32(b)
            for kc in range(KC):
                P.op("pe", lambda e, o=ps[:, 0:T1], l=win[:, kc, f * 128:(f + 1) * 128], r=hT[slot][:, kc, :], kc=kc:
                     e.matmul(o, l, r, start=(kc == 0), stop=(kc == KC - 1)),
                     win_keys(f * 128, f * 128 + 128) + hT_keys(slot), [("bank", b)])
            eng = "act" if rr_alt.next() == 0 else "dve"
            evac_copy(eng, kT[rs][:, ki, :], ps[:, 0:T1], [("bank", b)], [("kT", rs, ki)], mul=0.125)

    def stage_q(s):
        slot = s % 3
        qchunks = [QA // 128 + j for j in range(4)] + [QB // 128 + j for j in range(4)]
        for qi, f in enumerate(qchunks):
            b = rr_bank.next()
            ps = bank_f32(b)
            for kc in range(KC):
                P.op("pe", lambda e, o=ps[:, 0:T1], l=win[:, kc, f * 128:(f + 1) * 128], r=hT[slot][:, kc, :], kc=kc:
                     e.matmul(o, l, r, start=(kc == 0), stop=(kc == KC - 1)),
                     win_keys(f * 128, f * 128 + 128) + hT_keys(slot), [("bank", b)])
            eng = "act" if rr_alt.next() == 0 else "dve"
            evac_copy(eng, qT[:, qi, :], ps[:, 0:T1], [("bank", b)], [("qT", qi)])

    def b_tiles(i):
        if i in (0, 1):
            js = [0, 1, 2, 3]
            return [(j, ("s", m)) for m, j in enumerate(js)]
        if i in (NBLK - 2, NBLK - 1):
            js = [NBLK - 4, NBLK - 3, NBLK - 2, NBLK - 1]
            return [(j, ("s", m)) for m, j in enumerate(js)]
        return [(i + o, ("i", o + 2)) for o in range(-2, 3)]

    bmask_ctr = [0]
    special_ids = {0: 0, 1: 1, NBLK - 2: 2, NBLK - 1: 3}
    cur_special = [0]

    def ensure_special(i):
        sid = special_ids[i]
        if cur_special[0] != sid:
            mflat = maskBs.rearrange("p a b c -> p (a b c)")
            for c0 in range(0, 8 * 4 * 128, 1024):
                xi = rr_xp.next()
                dma_load(xp[xi], biasBs_d[sid][:, c0:c0 + 1024], f"ld_x{xi}", [("xp", xi)])
                P.op("act", lambda e, o=mflat[:, c0:c0 + 1024], i_=xp[xi]: e.activation(o, i_, AF.Exp),
                     [("xp", xi)], ["maskBs"])
            cur_special[0] = sid

    YBA = [0, 1]
    YBB = [2, 3]
    rr_sb = RR(2)
    rr_db = RR(2)
    SBK = [4, 5]
    DBK = [6, 7]

    def gen_attn(s):
        ytile = yT[s % 2]
        tasks = []
        for blk in range(NB1):
            i = s * NB1 + blk
            ja = [j for j in (i - 1, i, i + 1) if 0 <= j < NBLK]
            for g in range(2):
                for j in ja:
                    tasks.append(dict(kind="A", blk=blk, i=i, g=g, j=j, msel=j - i + 1,
                                      first=(j == ja[0]), last=(j == ja[-1]), grp_first=(g == 0 and j == ja[0]),
                                      grp_last=(g == 1 and j == ja[-1])))
            bt = b_tiles(i)
            for n, (j, msel) in enumerate(bt):
                for hbk in range(2):
                    tasks.append(dict(kind="B", blk=blk, i=i, g=hbk, j=j, msel=msel,
                                      first=(n == 0), last=(n == len(bt) - 1), grp_first=(n == 0 and hbk == 0),
                                      grp_last=(n == len(bt) - 1 and hbk == 1)))

        def emit_qk(t):
            g, j, blk = t["g"], t["j"], t["blk"]
            tq = slice(blk * 128, (blk + 1) * 128)
            rs = (j // NB1) % 3
            pos = j % NB1
            kcols = slice(pos * 128, (pos + 1) * 128)
            b = SBK[rr_sb.next()]
            ps = bank_f32(b)
            if t["kind"] == "A":
                pr = slice(64 * g, 64 * g + 64)
                P.op("pe", lambda e, o=ps.rearrange("p (a b) -> p a b", a=4), l=kT[rs][pr, 0, kcols], r=qT[pr, 0:4, tq]:
                     e.matmul(o, l, r, start=True, stop=True),
                     [("kT", rs, 0)] + [("qT", c) for c in range(4)], [("bank", b)])
            else:
                for hh in range(4):
                    h = 2 * hh + g
                    pr = slice(64 * (h % 2), 64 * (h % 2) + 64)
                    P.op("pe", lambda e, o=ps[:, hh * 128:(hh + 1) * 128], l=kT[rs][pr, 1 + h // 2, kcols], r=qT[pr, 4 + h // 2, tq]:
                         e.matmul(o, l, r, start=True, stop=True),
                         [("kT", rs, 1 + h // 2), ("qT", 4 + h // 2)], [("bank", b)])
            return b

        def emit_soft(t, b):
            g, msel = t["g"], t["msel"]
            ps = bank_f32(b)
            ui = rr_pu.next()
            ti = rr_pt.next()
            P.op("act", lambda e, o=pu[ui], i_=ps: e.activation(o, i_, AF.Exp), [("bank", b)], [("pu", ui)])
            if t["kind"] == "A":
                m_ap = maskA[:, 4 * g:4 * g + 4, msel, :]
                mkey = "maskA"
                eng = "dve"
            else:
                if msel[0] == "i":
                    m_ap = maskBi[:, g:8:2, msel[1], :]
                    mkey = "maskBi"
                else:
                    m_ap = maskBs[:, g:8:2, msel[1], :]
                    mkey = "maskBs"
                eng = "pool" if (bmask_ctr[0] % 5) != 4 else "dve"
                bmask_ctr[0] += 1
            P.op(eng, lambda e, o=pt[ti], a=pu[ui].rearrange("p (a b) -> p a b", a=4), m=m_ap:
                 e.tensor_tensor(o, a, m, ALU.mult), [("pu", ui), mkey], [("pt", ti)])
            return ti

        def ybanks(kind):
            return YBA if kind == "A" else YBB

        def emit_pv(t, ti):
            g, j = t["g"], t["j"]
            rs = (j // NB1) % 3
            pos = j % NB1
            yb = ybanks(t["kind"])[g]
            yps = bank_f32(yb)
            for c in range(4):
                vh = g if t["kind"] == "A" else 2 + 2 * c + g
                P.op("pe", lambda e, o=yps[:, c * 65:(c + 1) * 65], l=pt[ti][:, c, :], r=vR[rs][:, pos, vh, :]:
                     e.matmul(o, l, r, start=False, stop=t["last"], skip_group_check=True),
                     [("pt", ti), ("v", rs, pos)], [("bank", yb)])

        def emit_zero(kind):
            for yb_ in ybanks(kind):
                P.op("pe", lambda e, o=bank_f32(yb_)[:, 0:260]: e.matmul(o, zt[:, 0:128], zt[:, 0:260], start=True, stop=False,
                                                                          skip_group_check=True), ["zt"], [("bank", yb_)])

        def emit_norm(kind, ybuf):
            for g in range(2):
                yb = ybanks(kind)[g]
                yps = bank_f32(yb)[:, 0:260].rearrange("p (h d) -> p h d", h=4)
                di = rr_den.next()
                dn = den[:, di, :]
                if kind == "A":
                    P.op("dve", lambda e, o=dn, a=yps[:, :, 64], b_=es[:, 4 * g:4 * g + 4]:
                         e.tensor_tensor(o, a, b_, ALU.add), [("bank", yb), "es"], [("den", di)])
                    P.op("dve", lambda e, o=dn: e.reciprocal(o, o), [("den", di)], [("den", di)])
                    yo = ysb[ybuf][:, g * 256:(g + 1) * 256].rearrange("p (h d) -> p h d", h=4)
                else:
                    P.op("dve", lambda e, o=dn, a=yps[:, :, 64]: e.reciprocal(o, a), [("bank", yb)], [("den", di)])
                    yo = ysb[ybuf][:, 512:1024].rearrange("p (h d) -> p h d", h=8)[:, g:8:2, :]
                P.op("dve", lambda e, o=yo, a=yps[:, :, 0:64], r=dn:
                     e.tensor_tensor(o, a, r.rearrange("p (h o) -> p h o", o=1).broadcast_to([128, 4, 64]), ALU.mult),
                     [("bank", yb), ("den", di)], [("ysb", ybuf, kind)])

        def emit_ytrans(blk, i):
            ybuf = blk
            tq = slice(blk * 128, (blk + 1) * 128)
            if debug:
                P.dma("pool", f"dbg_y{ybuf}", lambda e, o=ydbg_d[i * 128:(i + 1) * 128, :], i_=ysb[ybuf]: e.dma_start(out=o, in_=i_),
                      reads=[("ysb", ybuf, "A"), ("ysb", ybuf, "B")], writes=[("ydbg", i)])
            b = SBK[rr_sb.next()]
            pst = bank_bf16(b)
            for fc in range(8):
                P.op("pe", lambda e, o=pst[:, fc * 128:(fc + 1) * 128], i_=ysb[ybuf][:, fc * 128:(fc + 1) * 128]:
                     e.transpose(o, i_, ident), [("ysb", ybuf, "A"), ("ysb", ybuf, "B"), "ident"], [("bank", b)])
            evac_copy("act", ytile[:, :, tq], pst.rearrange("p (a b) -> p a b", a=8), [("bank", b)], [("yT", s % 2, blk)])

        def retire(t, ti):
            if t["grp_first"]:
                emit_zero(t["kind"])
            emit_pv(t, ti)
            if t["grp_last"]:
                emit_norm(t["kind"], t["blk"])
                if t["kind"] == "B":
                    deferred.append([3, lambda blk=t["blk"], i=t["i"]: emit_ytrans(blk, i)])

        LOOK = 4
        pend = []
        deferred = []

        def tick():
            for dd in list(deferred):
                dd[0] -= 1
                if dd[0] <= 0:
                    deferred.remove(dd)
                    dd[1]()

        for t in tasks:
            if t["kind"] == "B" and t["grp_first"] and t["i"] in special_ids:
                ensure_special(t["i"])
            b = emit_qk(t)
            ti = emit_soft(t, b)
            pend.append((t, ti))
            yield
            tick()
            if len(pend) > LOOK:
                retire(*pend.pop(0))
        while pend:
            retire(*pend.pop(0))
            yield
            tick()
        while deferred:
            yield
            tick()

    def gen_proj(s):
        slot = s % 3
        ytile = yT[s % 2]
        yT_keys = [("yT", s % 2, b) for b in range(NB1)]
        for c in range(8):
            bg = DBK[rr_db.next()]
            psg = bank_f32(bg)
            for half, goff in enumerate((GA, GB)):
                for kc in range(KC):
                    P.op("pe", lambda e, o=psg[:, half * T1:(half + 1) * T1], l=win[:, kc, goff + c * 128:goff + (c + 1) * 128],
                         r=hT[slot][:, kc, :], kc=kc: e.matmul(o, l, r, start=(kc == 0), stop=(kc == KC - 1)),
                         win_keys(goff + c * 128, goff + (c + 1) * 128) + hT_keys(slot), [("bank", bg)])
                yield
            gi = rr_alt.next()
            for half in range(2):
                P.op("act", lambda e, o=sg[gi][:, half * T1:(half + 1) * T1], i_=psg[:, half * T1:(half + 1) * T1],
                     bb=nbgate[:, half * 8 + c:half * 8 + c + 1]: e.activation(o, i_, AF.Exp, bias=bb, scale=-1.0),
                     [("bank", bg), "nbgate"], [("sg", gi)])
            bu = DBK[rr_db.next()]
            psu = bank_f32(bu)
            for half, wp in enumerate((wpa, wpb)):
                for fc in range(4):
                    P.op("pe", lambda e, o=psu[:, half * T1:(half + 1) * T1], l=wp[:, fc, c * 128:(c + 1) * 128],
                         r=ytile[:, half * 4 + fc, :], fc=fc: e.matmul(o, l, r, start=(fc == 0), stop=(fc == 3)),
                         ["wpa", "wpb"] + yT_keys, [("bank", bu)])
                if half == 0:
                    yield
            P.op("act", lambda e, o=sg[gi]: e.activation(o, o, AF.Ln, bias=ones1[:, 0:1]), [("sg", gi), "ones1"], [("sg", gi)])
            P.op("act", lambda e, o=sg[gi]: e.activation(o, o, AF.Exp, scale=-1.0), [("sg", gi)], [("sg", gi)])
            P.op("dve", lambda e, o=sg[gi], a=sg[gi], u=psu: e.tensor_tensor(o, a, u, ALU.mult),
                 [("sg", gi), ("bank", bu)], [("sg", gi)])
            P.op("pool", lambda e, o=mT[:, c, :], a=sg[gi][:, 0:T1], b_=sg[gi][:, T1:2 * T1]:
                 e.tensor_tensor(o, a, b_, ALU.add), [("sg", gi)], [("mT", c)])
            yield

    def gen_out(s):
        mT_keys = [("mT", c) for c in range(8)]
        if debug:
            P.dma("pool", "dbg_m", lambda e, o=mdbg_d.rearrange("(c p) t -> p c t", p=128)[:, :, s * T1:(s + 1) * T1], i_=mT:
                  e.dma_start(out=o, in_=i_), reads=mT_keys, writes=[("mdbg", s)])
        yield
        yield
        for blk in range(NB1):
            r0 = s * T1 + blk * 128
            xi = rr_xp.next()
            dma_load(xp[xi], x_d[r0:r0 + 128, :], f"ld_x{xi}", [("xp", xi)])
            for half in range(2):
                b = DBK[rr_db.next()]
                ps = bank_f32(b)
                for c in range(8):
                    P.op("pe", lambda e, o=ps, l=mT[:, c, blk * 128:(blk + 1) * 128], r=wout[:, c, half * 512:(half + 1) * 512], c=c:
                         e.matmul(o, l, r, start=(c == 0), stop=(c == 7)), mT_keys + ["wout"], [("bank", b)])
                    if c == 3:
                        yield
                P.op("dve", lambda e, o=xp[xi][:, half * 512:(half + 1) * 512], a=ps:
                     e.tensor_tensor(o, a, o, ALU.add), [("bank", b), ("xp", xi)], [("xp", xi)])
                yield
            P.dma("sp", f"st_x{xi}", lambda e, o=x1_d[r0:r0 + 128, :], i_=xp[xi]: e.dma_start(out=o, in_=i_),
                  reads=[("xp", xi)], writes=[("x1", r0 // 128)])

    def gen_chain(*gens):
        for g_ in gens:
            yield from g_

    def interleave(ga, gd):
        a_done = d_done = False
        while not (a_done and d_done):
            if not a_done:
                try:
                    next(ga)
                except StopIteration:
                    a_done = True
            if not d_done:
                try:
                    next(gd)
                except StopIteration:
                    d_done = True

    stage_prep1(0)
    stage_trans1(0)
    stage_prep1(1)
    stage_kv(0)
    for s in range(NS1):
        if s + 1 < NS1:
            stage_trans1(s + 1)
            if s + 2 < NS1:
                stage_prep1(s + 2)
            stage_kv(s + 1)
        stage_q(s)
        dense = gen_chain(gen_proj(s - 1), gen_out(s - 1)) if s >= 1 else iter(())
        interleave(gen_attn(s), dense)
    interleave(iter(()), gen_chain(gen_proj(NS1 - 1), gen_out(NS1 - 1)))
    if stop_after is not None:
        P.wait_all("sp")
        P.replay(nc)
        return nc

    P.barrier()
    AR.reset(base_mark)
    wup = AR.alloc(BF16, [KC, DFF])
    wdn = AR.alloc(BF16, [32, D])
    NXB = 5
    xb = [AR.alloc(F32, [D]) for _ in range(NXB)]
    NHB2 = 4
    hb2 = [AR.alloc(BF16, [D]) for _ in range(NHB2)]
    junk2 = AR.alloc(BF16, [D])
    h2T = AR.alloc(BF16, [KC, T2])
    aT = AR.alloc(BF16, [32, T2])
    NRT = 3
    rt = [AR.alloc(BF16, [T2]) for _ in range(NRT)]
    gfin = AR.alloc(F32, [D])
    rr_xb = RR(NXB)
    rr_bank = RR(8)
    rr_rt = RR(NRT)
    rr_hb2 = RR(NHB2)

    dma_load(gfin, gfin_d.partition_broadcast(128), "c_gfin", ["gfin"])
    wup_v = wup_d.rearrange("(kc p) n -> p kc n", p=128)
    for cb in range(4):
        c0 = cb * 1024
        for kc in range(KC):
            xi = rr_xb.next()
            dma_load(xb[xi], wup_v[:, kc, c0:c0 + 1024], f"ld_b{xi}", [("xb", xi)])
            eng = "dve" if rr_alt.next() == 0 else "act"
            P.op(eng, fold_scale(eng, wup[:, kc, c0:c0 + 1024], xb[xi], gmlp[:, kc:kc + 1]),
                 [("xb", xi), "gmlp"], [("wup", cb)])

    for fc in range(32):
        P.dma("pool", f"w_d{fc // 4}", lambda e, fc=fc: e.dma_start(out=wdn[:, fc, :], in_=wdn_d[fc * 128:(fc + 1) * 128, :]), reads=[("wup", 2)], writes=[("wdn", fc // 4)])
    hb2keys = "hb2"

    def prep2(u, blk):
        r0 = u * T2 + blk * 128
        xi = rr_xb.next()
        dma_load(xb[xi], x1_d[r0:r0 + 128, :], f"ld_b{xi}", [("xb", xi)], rkeys=[("x1", r0 // 128)])
        norm_prep2(xb[xi], ("xb", xi), ("p2", u, blk))
        return xi

    def norm_prep2(x_ap, x_key, tag):
        hi = rr_hb2.next()
        si = rr_ss.next()
        P.op("act", lambda e, o=hb2[hi], i_=x_ap, a=ss[:, si:si + 1]: e.activation(o, i_, AF.Square, accum_out=a),
             [x_key], [("hb2", hi), ("ss", si)])
        emit_rstd(si)
        P.op("act", lambda e, o=hb2[hi], i_=x_ap, m=rstd[:, si:si + 1]: e.mul(o, i_, m),
             [x_key, ("rstd", si)], [("hb2", hi)])
        prep_state[tag] = hi

    def trans2(u, blk):
        hi = prep_state.pop(("p2", u, blk))
        b = rr_bank.next()
        pst = bank_bf16(b)
        for kc in range(KC):
            P.op("pe", lambda e, o=pst[:, kc * 128:(kc + 1) * 128], i_=hb2[hi][:, kc * 128:(kc + 1) * 128]:
                 e.transpose(o, i_, ident), [("hb2", hi), "ident"], [("bank", b)])
        eng = "dve" if rr_alt.next() == 0 else "act"
        evac_copy(eng, h2T[:, :, blk * 128:(blk + 1) * 128], pst.rearrange("p (a b) -> p a b", a=KC),
                  [("bank", b)], [("h2T", blk)])

    h2T_keys = [("h2T", b) for b in range(NB2)]
    xis = {}
    for blk in range(NB2):
        xis[(0, blk)] = prep2(0, blk)
    for blk in range(NB2):
        trans2(0, blk)
    for u in range(NS2):
        for fc in range(32):
            b = rr_bank.next()
            ps = bank_f32(b)
            for kc in range(KC):
                P.op("pe", lambda e, o=ps, l=wup[:, kc, fc * 128:(fc + 1) * 128], r=h2T[:, kc, :], kc=kc:
                     e.matmul(o, l, r, start=(kc == 0), stop=(kc == KC - 1)), [("wup", fc // 8)] + h2T_keys, [("bank", b)])
            ri = rr_rt.next()
            P.op("act", lambda e, o=rt[ri], i_=ps: e.activation(o, i_, AF.Relu), [("bank", b)], [("rt", ri)])
            P.op("dve", lambda e, o=aT[:, fc, :], a=rt[ri]: e.tensor_tensor(o, a, a, ALU.mult), [("rt", ri)], [("aT", fc)])
        if u + 1 < NS2:
            xis[(u + 1, 0)] = prep2(u + 1, 0)
        for blk in range(NB2):
            xi = xis[(u, blk)]
            r0 = u * T2 + blk * 128
            for half in range(2):
                b = rr_bank.next()
                ps = bank_f32(b)
                for fc in range(32):
                    P.op("pe", lambda e, o=ps, l=aT[:, fc, blk * 128:(blk + 1) * 128], r=wdn[:, fc, half * 512:(half + 1) * 512], fc=fc:
                         e.matmul(o, l, r, start=(fc == 0), stop=(fc == 31)), [("aT", fc), ("wdn", fc // 4)], [("bank", b)])
                P.op("dve", lambda e, o=xb[xi][:, half * 512:(half + 1) * 512], a=ps:
                     e.tensor_tensor(o, a, o, ALU.add), [("bank", b), ("xb", xi)], [("xb", xi)])
            si = rr_ss.next()
            P.op("act", lambda e, o=junk2, i_=xb[xi], a=ss[:, si:si + 1]: e.activation(o, i_, AF.Square, accum_out=a),
                 [("xb", xi)], ["junk2", ("ss", si)])
            emit_rstd(si)
            P.op("dve", lambda e, o=xb[xi], m=rstd[:, si:si + 1]:
                 e.scalar_tensor_tensor(o, o, m, gfin, ALU.mult, ALU.mult), [("xb", xi), ("rstd", si), "gfin"], [("xb", xi)])
            P.dma("sp", f"st_b{xi}", lambda e, o=out_d[r0:r0 + 128, :], i_=xb[xi]: e.dma_start(out=o, in_=i_),
                  reads=[("xb", xi)], writes=[("out", r0 // 128)])
            if u + 1 < NS2 and blk + 1 < NB2:
                xis[(u + 1, blk + 1)] = prep2(u + 1, blk + 1)
        if u + 1 < NS2:
            for blk in range(NB2):
                trans2(u + 1, blk)
    P.wait_all("sp")
    P.replay(nc)
    return nc


def _perm_cols():
    cols = []
    for c in range(4):
        cols += list(range(c * 64, (c + 1) * 64))
        cols += list(range((4 + c) * 64, (5 + c) * 64))
    cols += list(range(512, 640))
    cols += list(range(768, 1280))
    cols += list(range(1280, 1792))
    cols += list(range(640, 768))
    cols += list(range(1792, 2304))
    cols += list(range(2304, 4352))
    return np.array(cols)


def _bias_a():
    k = np.arange(128)[:, None]
    q = np.arange(128)[None, :]
    out = np.full((128, 8, 3, 128), NEG, np.float32)
    for h in range(8):
        slope = 2.0 ** (-(h + 1))
        for sl in range(3):
            dist = np.abs((sl - 1) * 128 + k - q)
            out[:, h, sl, :] = np.where(dist <= 128, -slope * dist, NEG)
    return out.reshape(128, -1)


def _bias_b_gather(rpb, rows, i, js):
    out = np.full((128, 8, len(js), 128), NEG, np.float32)
    kk = np.arange(128)
    a, ck = kk // 64, kk % 64
    b, cq = kk // 64, kk % 64
    cs = np.clip(cq - 8, 0, 48)
    r = 2 * i + b
    r0 = np.clip(r - 4, 0, rows - 8)
    for m, j in enumerate(js):
        R = 2 * j + a
        vr = (R[:, None] >= r0[None, :]) & (R[:, None] <= r0[None, :] + 7)
        vc = (ck[:, None] >= cs[None, :]) & (ck[:, None] <= cs[None, :] + 15)
        valid = vr & vc
        dr = np.clip(R[:, None] - r[None, :] + 7, 0, 14)
        dc = np.clip(ck[:, None] - cq[None, :] + 15, 0, 30)
        g = rpb[:, dr, dc]
        g = np.where(valid[None], g, np.float32(NEG))
        out[:, :, m, :] = np.transpose(g, (1, 0, 2))
    return out


_NC_CACHE = {}


def _get_nc(S, debug=False):
    key = (S, debug)
    if key not in _NC_CACHE:
        _NC_CACHE[key] = build_program(S, debug)
    return _NC_CACHE[key]


def make_in_maps(x, norm_mix, w_in, b_gate, sink, rpb, w_proj_a, w_proj_b, w_out,
                 norm_mlp, w_up, w_down, norm_final):
    f = lambda a: np.ascontiguousarray(np.asarray(a, dtype=np.float32))
    x = f(x)
    B, S, _ = x.shape
    rows = S // 64
    NBLK = S // 128
    rpb0 = f(rpb)[0]
    common = {
        "w_in": f(f(w_in)[0][:, _perm_cols()]),
        "w_proj_a": f(w_proj_a)[0], "w_proj_b": f(w_proj_b)[0], "w_out": f(w_out)[0],
        "w_up": f(w_up)[0], "w_down": f(w_down)[0],
        "gmix": f(f(norm_mix)[0].reshape(KC, 128).T),
        "gmlp": f(f(norm_mlp)[0].reshape(KC, 128).T),
        "bgate": f(f(b_gate)[0].reshape(16, 128).T),
        "sink": f(sink)[0], "gfin": f(norm_final),
        "ident": np.eye(128, dtype=np.float32),
        "biasA": _bias_a(),
        "biasBi": f(_bias_b_gather(rpb0, rows, 2, [0, 1, 2, 3, 4]).reshape(128, -1)),
        "biasBs": f(np.stack([
            _bias_b_gather(rpb0, rows, 0, [0, 1, 2, 3]).reshape(128, -1),
            _bias_b_gather(rpb0, rows, 1, [0, 1, 2, 3]).reshape(128, -1),
            _bias_b_gather(rpb0, rows, NBLK - 2, [NBLK - 4, NBLK - 3, NBLK - 2, NBLK - 1]).reshape(128, -1),
            _bias_b_gather(rpb0, rows, NBLK - 1, [NBLK - 4, NBLK - 3, NBLK - 2, NBLK - 1]).reshape(128, -1),
        ])),
    }
    return [dict(common, x=f(x[b])) for b in range(B)]


def kernel(x, norm_mix, w_in, b_gate, sink, rpb, w_proj_a, w_proj_b, w_out,
           norm_mlp, w_up, w_down, norm_final):
    in_maps = make_in_maps(x, norm_mix, w_in, b_gate, sink, rpb, w_proj_a, w_proj_b, w_out,
                           norm_mlp, w_up, w_down, norm_final)
    B = len(in_maps)
    S = in_maps[0]["x"].shape[0]
    nc = _get_nc(S)
    res = run_bass_kernel_spmd(nc, in_maps, core_ids=list(range(B)))
    return np.stack([np.asarray(r["out"], dtype=np.float32) for r in res.results], axis=0)
```
